# Optimizing a Trainium2 kernel written in Bass

```python
import math
import jax
import jax.numpy as jnp
from jax import lax
import numpy as np

D_MODEL = 1024
BATCH = 4
SEQ = 4096
DEPTH = 1

CHUNK = 64
CONV_CH = D_MODEL
CONV_K = 31
D_INNER = 2 * D_MODEL
HEAD_DIM = 64
N_HEADS = D_INNER // HEAD_DIM
N_GROUPS = 8
HEADS_PER_GROUP = N_HEADS // N_GROUPS
D_STATE = 128
SSM_CONV_K = 4
XBC_CH = D_INNER + 2 * N_GROUPS * D_STATE
FFN_HIDDEN = ((8 * D_MODEL // 3 + 255) // 256) * 256
RMS_EPS = 1e-6
LN_EPS = 1e-5

OFF_Z = 2 * CONV_CH
OFF_XBC = OFF_Z + D_INNER
OFF_DT = OFF_XBC + XBC_CH
OFF_GATE = OFF_DT + N_HEADS
IN_COLS = OFF_GATE + 2 * D_MODEL

kernel_name = "hybrid_conformer_ssd_block"


def rms_norm(x, g):
    xf = x.astype(jnp.float32)
    y = xf * lax.rsqrt(jnp.mean(xf * xf, axis=-1, keepdims=True) + RMS_EPS)
    return (y * g.astype(jnp.float32)).astype(x.dtype)


def layer_norm(x, g, b):
    xf = x.astype(jnp.float32)
    mu = jnp.mean(xf, axis=-1, keepdims=True)
    xc = xf - mu
    y = xc * lax.rsqrt(jnp.mean(xc * xc, axis=-1, keepdims=True) + LN_EPS)
    return (y * g.astype(jnp.float32) + b.astype(jnp.float32)).astype(x.dtype)


def causal_depthwise_conv(u, w, b):
    k, ch = w.shape
    y = lax.conv_general_dilated(
        u, w[:, None, :], window_strides=(1,), padding=[(k - 1, 0)],
        dimension_numbers=("NWC", "WIO", "NWC"), feature_group_count=ch)
    return y + b


def conformer_conv_branch(u_glu, glu_b, dw_w, dw_b, ln_g, ln_b, pw_w, pw_b):
    u = u_glu + glu_b
    val, gt = jnp.split(u, 2, axis=-1)
    v = val * jax.nn.sigmoid(gt)
    v = causal_depthwise_conv(v, dw_w, dw_b)
    v = jax.nn.silu(layer_norm(v, ln_g, ln_b))
    return v @ pw_w + pw_b


def ssd_chunked(x, dt, a, b_mat, c_mat, d_skip):
    bsz, seqlen = x.shape[0], x.shape[1]
    nc = seqlen // CHUNK
    G, R = N_GROUPS, HEADS_PER_GROUP
    x = x.reshape(bsz, nc, CHUNK, G, R, HEAD_DIM)
    dt = dt.reshape(bsz, nc, CHUNK, G, R)
    b_mat = b_mat.reshape(bsz, nc, CHUNK, G, D_STATE)
    c_mat = c_mat.reshape(bsz, nc, CHUNK, G, D_STATE)
    a_dt = jnp.transpose(dt * a.reshape(G, R), (0, 3, 4, 1, 2))
    a_cs = jnp.cumsum(a_dt, axis=-1)
    xdt = x * dt[..., None]
    causal = jnp.tril(jnp.ones((CHUNK, CHUNK), dtype=bool))
    seg = a_cs[..., :, None] - a_cs[..., None, :]
    decay_in = jnp.exp(jnp.where(causal, seg, -jnp.inf))
    cb = jnp.einsum("bclgn,bcsgn->bgcls", c_mat, b_mat)
    y_diag = jnp.einsum("bgcls,bgrcls,bcsgrp->bclgrp", cb, decay_in, xdt)
    decay_to_end = jnp.exp(a_cs[..., -1:] - a_cs)
    chunk_states = jnp.einsum("bclgn,bgrcl,bclgrp->bcgrpn", b_mat, decay_to_end, xdt)
    chunk_decay = jnp.exp(a_cs[..., -1])

    def step(carry, inp):
        s_c, dec_c = inp
        return carry * dec_c[..., None, None] + s_c, carry

    init = jnp.zeros((bsz, G, R, HEAD_DIM, D_STATE), jnp.float32)
    _, prev = lax.scan(step, init, (jnp.moveaxis(chunk_states, 1, 0),
                                    jnp.moveaxis(chunk_decay, -1, 0)))
    prev = jnp.moveaxis(prev, 0, 1)
    y_off = jnp.einsum("bclgn,bcgrpn,bgrcl->bclgrp", c_mat, prev, jnp.exp(a_cs))
    y = y_diag + y_off + x * d_skip.reshape(G, R)[..., None]
    return y.reshape(bsz, seqlen, N_HEADS * HEAD_DIM)


def ssd_branch(z, xbc, dt_raw, conv_w, conv_b, dt_bias, a_log, d_skip, norm_g, out_w):
    f32 = jnp.float32
    bsz, seqlen = z.shape[0], z.shape[1]
    xbc = jax.nn.silu(causal_depthwise_conv(xbc, conv_w, conv_b))
    gn = N_GROUPS * D_STATE
    xs = xbc[..., :D_INNER]
    bm = xbc[..., D_INNER:D_INNER + gn]
    cm = xbc[..., D_INNER + gn:]
    dt = jax.nn.softplus(dt_raw.astype(f32) + dt_bias.astype(f32))
    a = -jnp.exp(a_log.astype(f32))
    y = ssd_chunked(xs.astype(f32).reshape(bsz, seqlen, N_HEADS, HEAD_DIM), dt, a,
                    bm.astype(f32).reshape(bsz, seqlen, N_GROUPS, D_STATE),
                    cm.astype(f32).reshape(bsz, seqlen, N_GROUPS, D_STATE),
                    d_skip.astype(f32))
    y = y * jax.nn.silu(z.astype(f32))
    yg = y.reshape(bsz, seqlen, N_GROUPS, D_INNER // N_GROUPS)
    yg = yg * lax.rsqrt(jnp.mean(yg * yg, axis=-1, keepdims=True) + RMS_EPS)
    y = (yg.reshape(bsz, seqlen, D_INNER) * norm_g.astype(f32)).astype(z.dtype)
    return y @ out_w


def setup_inputs(seed: int = 0) -> dict:
    key = jax.random.key(seed)
    ks = jax.random.split(key, 24)
    f32 = jnp.float32

    def nrm(k, shape, scale):
        return jax.random.normal(k, shape, f32) * scale

    def gain(k, n):
        return 1.0 + 0.05 * jax.random.normal(k, (DEPTH, n), f32)

    u = jax.random.uniform(ks[13], (DEPTH, N_HEADS), f32)
    dt0 = jnp.exp(u * (math.log(0.1) - math.log(1e-3)) + math.log(1e-3))
    dt_bias = dt0 + jnp.log(-jnp.expm1(-dt0))
    a_log = jnp.log(jax.random.uniform(ks[14], (DEPTH, N_HEADS), f32, 1.0, 16.0))
    return {
        "x": nrm(ks[0], (BATCH, SEQ, D_MODEL), 1.0),
        "mix_pre_g": gain(ks[1], D_MODEL),
        "w_in": nrm(ks[2], (DEPTH, D_MODEL, IN_COLS), D_MODEL ** -0.5),
        "gate_b": nrm(ks[3], (DEPTH, 2 * D_MODEL), 0.02),
        "glu_b": nrm(ks[4], (DEPTH, 2 * CONV_CH), 0.02),
        "conv_dw_w": nrm(ks[5], (DEPTH, CONV_K, CONV_CH), CONV_K ** -0.5),
        "conv_dw_b": nrm(ks[6], (DEPTH, CONV_CH), 0.02),
        "conv_ln_g": gain(ks[7], CONV_CH),
        "conv_ln_b": nrm(ks[8], (DEPTH, CONV_CH), 0.02),
        "conv_pw_w": nrm(ks[9], (DEPTH, CONV_CH, D_MODEL), CONV_CH ** -0.5),
        "conv_pw_b": nrm(ks[10], (DEPTH, D_MODEL), 0.02),
        "ssm_conv_w": nrm(ks[11], (DEPTH, SSM_CONV_K, XBC_CH), SSM_CONV_K ** -0.5),
        "ssm_conv_b": nrm(ks[12], (DEPTH, XBC_CH), 0.02),
        "dt_bias": dt_bias,
        "a_log": a_log,
        "d_skip": 1.0 + 0.1 * jax.random.normal(ks[15], (DEPTH, N_HEADS), f32),
        "ssm_norm_g": gain(ks[16], D_INNER),
        "ssm_out_w": nrm(ks[17], (DEPTH, D_INNER, D_MODEL), D_INNER ** -0.5),
        "w_out": nrm(ks[18], (DEPTH, D_MODEL, D_MODEL), D_MODEL ** -0.5),
        "mix_post_g": gain(ks[19], D_MODEL),
        "ffn_pre_g": gain(ks[20], D_MODEL),
        "w_gate_up": nrm(ks[21], (DEPTH, D_MODEL, 2 * FFN_HIDDEN), D_MODEL ** -0.5),
        "w_down": nrm(ks[22], (DEPTH, FFN_HIDDEN, D_MODEL), FFN_HIDDEN ** -0.5),
        "ffn_post_g": gain(ks[23], D_MODEL),
    }


def reference(x, mix_pre_g, w_in, gate_b, glu_b, conv_dw_w, conv_dw_b, conv_ln_g,
              conv_ln_b, conv_pw_w, conv_pw_b, ssm_conv_w, ssm_conv_b, dt_bias, a_log,
              d_skip, ssm_norm_g, ssm_out_w, w_out, mix_post_g, ffn_pre_g, w_gate_up,
              w_down, ffn_post_g):
    for i in range(DEPTH):
        h = rms_norm(x, mix_pre_g[i])
        proj = h @ w_in[i]
        u_glu = proj[..., :OFF_Z]
        z = proj[..., OFF_Z:OFF_XBC]
        xbc = proj[..., OFF_XBC:OFF_DT]
        dt_raw = proj[..., OFF_DT:OFF_GATE]
        gates = jax.nn.sigmoid(proj[..., OFF_GATE:] + gate_b[i])
        g_conv, g_ssm = jnp.split(gates, 2, axis=-1)
        conv_out = conformer_conv_branch(u_glu, glu_b[i], conv_dw_w[i], conv_dw_b[i],
                                         conv_ln_g[i], conv_ln_b[i], conv_pw_w[i],
                                         conv_pw_b[i])
        ssm_out = ssd_branch(z, xbc, dt_raw, ssm_conv_w[i], ssm_conv_b[i], dt_bias[i],
                             a_log[i], d_skip[i], ssm_norm_g[i], ssm_out_w[i])
        mixed = (g_conv * conv_out + g_ssm * ssm_out) @ w_out[i]
        x = x + rms_norm(mixed, mix_post_g[i])
        h = rms_norm(x, ffn_pre_g[i])
        gt, up = jnp.split(h @ w_gate_up[i], 2, axis=-1)
        f = (jax.nn.silu(gt) * up) @ w_down[i]
        x = x + rms_norm(f, ffn_post_g[i])
    return x
```

```python
import numpy as np
from contextlib import ExitStack
import concourse.bass as bass
import concourse.mybir as mybir
from concourse.bass_utils import run_bass_kernel_spmd

F32 = mybir.dt.float32
BF16 = mybir.dt.bfloat16
AF = mybir.ActivationFunctionType
ALU = mybir.AluOpType

D = 1024
T = 512
OFF_Z = 2048
OFF_XBC = 4096
OFF_DT = 8192
OFF_GATE = 8224
IN_COLS = 10272
FH = 2816
P_GATEB, P_GLUB, P_DWW, P_DWB, P_LNG, P_LNB, P_PWB = 0, 16, 32, 280, 288, 296, 304
P_SCW, P_SCB, P_NG, P_FLAG, P_PREG, P_FPREG, NPP = 312, 440, 472, 488, 489, 497, 512
B_POSTG, B_FPOSTG, B_DTB, B_ALOG, B_DSK, NBC = 0, 1024, 2048, 2080, 2112, 2144
NPRE = 4
NMAIN = 4
CUT = 99
LAST_SIM = None
DEBUG_SCHED = False
DMA_BW = 300.0
SCHED_WIN = 48
USE_WCACHE = True
PRIO_MODE = 2
PRIO_ALPHA = 0.05


class _Call:
    def __init__(self, name, a, kw):
        self.name, self.a, self.kw = name, a, kw


class _Rec:
    def __getattr__(self, name):
        return lambda *a, **kw: _Call(name, a, kw)


def _free(ap):
    n = 1
    for d in ap.shape[1:]:
        n *= d
    return n


class Sched:
    EXCL = set([('acc', 0), ('acc', 1), ('acc', 2), ('ptb', 0), ('ptb', 1), 'S0', 'S1', 'S2'])

    def __init__(self, nc, es, ndma=8):
        self.nc = nc
        self.eng = {'pe': nc.tensor, 'act': nc.scalar, 'dve': nc.vector,
                    'pool': nc.gpsimd, 'sp': nc.sync}
        self.sem = {k: es.enter_context(nc.semaphore('s_' + k)) for k in self.eng}
        self.dsem = {}
        for q in ('sp', 'pool'):
            self.dsem[q] = [es.enter_context(nc.semaphore(f'd_{q}{i}')) for i in range(ndma)]
        self.ops = []
        self.lastw = {}
        self.readers = {}
        self.rec = _Rec()
        self.tag = 'init'
        self.alias = {}

    def _mkdeps(self, eng, reads, writes):
        deps = set()
        why = {}
        for r in reads:
            w = self.lastw.get(r)
            if w is not None:
                deps.add(w)
                why[w] = ('RAW', r)
            if r in self.EXCL:
                for i in self.readers.get(r, ()):
                    if self.ops[i]['eng'] != eng:
                        deps.add(i)
                        why[i] = ('XRD', r)
        for w_ in writes:
            w = self.lastw.get(w_)
            if w is not None:
                deps.add(w)
                why.setdefault(w, ('WAW', w_))
            for i in self.readers.get(w_, ()):
                deps.add(i)
                why.setdefault(i, ('WAR', w_))
        self._why = why
        return deps

    def _note(self, idx, reads, writes):
        for r in reads:
            self.readers.setdefault(r, []).append(idx)
        for w in writes:
            self.lastw[w] = idx
            self.readers[w] = []

    def _est(self, eng, call):
        kw = call.kw
        if eng == 'pe':
            if call.name == 'transpose':
                return 110.0
            rhs = kw.get('rhs')
            n = _free(rhs)
            t = 25.0 + n / 2.0
            if rhs.dtype == F32:
                t *= 4
            return max(t, 35.0)
        out = kw.get('out') if 'out' in kw else (call.a[0] if call.a else None)
        n = _free(out) if out is not None else 64
        if eng == 'act':
            return 230.0 + n / 1.2
        if eng == 'dve':
            return 120.0 + n / 0.96
        return 600.0 + n * 7.0

    def _expand(self, keys):
        out = []
        for k in keys:
            if k in self.alias:
                out.extend(self.alias[k])
            else:
                out.append(k)
        return out

    def op(self, eng, fn, reads=(), writes=()):
        reads, writes = self._expand(reads), self._expand(writes)
        call = fn(self.rec)
        idx = len(self.ops)
        deps = self._mkdeps(eng, reads, writes)
        t = self._est(eng, call)
        tset = None
        if eng == 'act':
            f = call.kw.get('func')
            tset = {AF.Exp: 'exp', AF.Ln: 'ln', AF.Silu: 'silu', AF.Tanh: 'silu', AF.Sigmoid: 'sig', AF.Sqrt: 'sqrt'}.get(f)
        self.ops.append(dict(eng=eng, call=call, deps=deps, occ=t, lat=t + 60.0, dma=False, tag=self.tag, tset=tset, why=self._why))
        self._note(idx, reads, writes)

    def dma(self, q, out, in_, reads=(), writes=(), nbytes=None, dep_writes=()):
        reads, writes = self._expand(reads), self._expand(writes)
        idx = len(self.ops)
        deps = self._mkdeps(q, reads, list(writes) + [k for k in self._expand(dep_writes) if k not in writes])
        if nbytes is None:
            nbytes = 4 * 128 * _free(out) if out.shape[0] == 128 else 4 * out.shape[0] * _free(out)
        self.ops.append(dict(eng=q, call=_Call('dma_start', (), dict(out=out, in_=in_)), deps=deps,
                             occ=(900.0 if q == 'pool' else 100.0), lat=2000.0 + nbytes / DMA_BW, dma=True, nbytes=nbytes, tag=self.tag, why=self._why))
        self._note(idx, reads, writes)

    def finish(self, eng='sp', reorder=True):
        ops = self.ops
        n = len(ops)
        succ = [[] for _ in range(n)]
        ndep = [0] * n
        for i, o in enumerate(ops):
            ndep[i] = len(o['deps'])
            for d in o['deps']:
                succ[d].append(i)
        ready = {k: [] for k in self.eng}
        for i in range(n):
            if ndep[i] == 0:
                ready[ops[i]['eng']].append(i)
        blev = [0.0] * n
        for i in range(n - 1, -1, -1):
            m = 0.0
            for s_ in succ[i]:
                if blev[s_] > m:
                    m = blev[s_]
            blev[i] = m + ops[i]['lat']
        free_t = {k: 0.0 for k in self.eng}
        fin = [0.0] * n
        order = []
        dma_pipe = 0.0
        WIN = SCHED_WIN
        last_on = {}
        cur_set = None
        while len(order) < n:
            best = None
            for k, lst in ready.items():
                if not lst:
                    continue
                lst.sort()
                cand = None
                for i in (lst[:WIN] if reorder else lst[:1]):
                    dr = 0.0
                    for d in ops[i]['deps']:
                        if k == 'pe' and ops[d]['eng'] == 'pe':
                            continue
                        f = fin[d] + (0.0 if ops[d]['eng'] == k else 150.0)
                        if f > dr:
                            dr = f
                    st = max(free_t[k], dr)
                    ts_ = ops[i].get('tset')
                    if ts_ is not None and ts_ != cur_set:
                        st += 1400.0
                    if PRIO_MODE == 1:
                        key_ = (st, -blev[i], i)
                    elif PRIO_MODE == 2:
                        key_ = (st - PRIO_ALPHA * blev[i], i)
                    else:
                        key_ = (st, i)
                    if cand is None or key_ < cand[2]:
                        cand = (st, i, key_)
                if best is None or cand[:2] < best[0][:2]:
                    best = (cand, k)
            (st, i, _k), k = best
            if not reorder:
                st = 0.0
                for d in ops[i]['deps']:
                    st = max(st, fin[d])
                st = max(st, free_t[k])
            o = ops[i]
            ready[k].remove(i)
            if o.get('tset') is not None:
                cur_set = o['tset']
            if DEBUG_SCHED:
                bd, bt = None, -1.0
                for d in o['deps']:
                    f = fin[d] + (0.0 if ops[d]['eng'] == k else 150.0)
                    if f > bt:
                        bd, bt = d, f
                if free_t[k] >= bt:
                    o['bind'] = ('eng', last_on.get(k))
                else:
                    o['bind'] = ('dep', bd)
                o['st'] = st
                last_on[k] = i
            free_t[k] = st + o['occ']
            if o['dma']:
                done = max(st + 2000.0, dma_pipe) + o["nbytes"] / DMA_BW
                dma_pipe = done
                fin[i] = done
            else:
                fin[i] = st + o['lat']
            order.append(i)
            for s_ in succ[i]:
                ndep[s_] -= 1
                if ndep[s_] == 0:
                    ready[ops[s_]['eng']].append(s_)
        self.sim_end = max(fin) if fin else 0.0
        global LAST_SIM
        LAST_SIM = dict(ops=ops if DEBUG_SCHED else None, fin=fin, end=self.sim_end, busy={k: sum(o['occ'] for o in ops if o['eng'] == k) for k in self.eng})
        cnt = {k: 0 for k in self.eng}
        dcnt = {(q, i): 0 for q in self.dsem for i in range(len(self.dsem[q]))}
        drr = {q: 0 for q in self.dsem}
        known = {k: {} for k in self.eng}
        ev = [None] * n

        def semof(key):
            return self.sem[key] if isinstance(key, str) else self.dsem[key[0]][key[1]]
        for i in order:
            o = ops[i]
            k = o['eng']
            e = self.eng[k]
            need = {}
            if o['dma']:
                slot_i = drr[k]
                drr[k] = (slot_i + 1) % len(self.dsem[k])
                dkey = (k, slot_i)
                if dcnt[dkey] > 0:
                    need[dkey] = dcnt[dkey]
            for d in o['deps']:
                sk, c = ev[d]
                if sk == 'pe' and k == 'pe':
                    continue
                if need.get(sk, 0) < c:
                    need[sk] = c
            for sk, c in need.items():
                if known[k].get(sk, 0) >= c:
                    continue
                e.wait_ge(semof(sk), c * (1 if isinstance(sk, str) else 16))
                known[k][sk] = c
            inst = getattr(e, o['call'].name)(*o['call'].a, **o['call'].kw)
            if o['dma']:
                inst.then_inc(self.dsem[k][slot_i], 16)
                dcnt[dkey] += 1
                ev[i] = (dkey, dcnt[dkey])
            else:
                cnt[k] += 1
                inst.then_inc(self.sem[k], 1)
                ev[i] = (k, cnt[k])
        for k in self.eng:
            if cnt[k] and k != eng:
                self.eng[eng].wait_ge(self.sem[k], cnt[k])
        for dkey, c in dcnt.items():
            if c:
                self.eng[eng].wait_ge(self.dsem[dkey[0]][dkey[1]], c * 16)


def build_program(npre=NPRE, nmain=NMAIN, dbg=None, upto='E'):
    nc = bass.Bass("TRN2", target_bir_lowering=False)

    def din(name, shape):
        return nc.dram_tensor(name, shape, F32, kind="ExternalInput").ap()
    xin = din("xin", [4096, D])
    w_in = din("w_in", [D, IN_COLS])
    w_pw = din("w_pw", [D, D])
    w_so = din("w_so", [2048, D])
    w_o = din("w_o", [D, D])
    w_gu = din("w_gu", [D, 2 * FH])
    w_dn = din("w_dn", [FH, D])
    ppd = din("pp", [128, NPP])
    bcd = din("bc", [128, NBC])
    outd = nc.dram_tensor("out", [2048, D], F32, kind="ExternalOutput").ap()
    dbg_out = {}
    if dbg:
        for name, shape in dbg.items():
            dbg_out[name] = nc.dram_tensor("dbg_" + name, shape, F32, kind="ExternalOutput").ap()

    es = ExitStack()
    with es:
        S = Sched(nc, es)

        def A(name, shape, dt=F32):
            return nc.alloc_sbuf_tensor("sb_" + name, shape, dt)
        identf = A("identf", [128, 128])
        identb = A("identb", [128, 128], BF16)
        U = A("U", [128, 128])
        LT = A("LT", [128, 128])
        onesf = A("onesf", [128, 128])
        pp = A("pp", [128, NPP])
        bc = A("bc", [128, NBC])
        abc = A("abc", [128, 32])
        slots = A("slots", [128, 32 * 512], BF16)
        WBN = 5632
        wbuf = [A(f"wb{i}", [128, WBN], BF16) for i in range(3)]
        xst = [A(f"xst{i}", [128, D]) for i in range(2)]
        x1 = A("x1", [128, 4, D])
        hb = A("hb", [128, D], BF16)
        junk = A("junk", [128, D], BF16)
        ss = A("ss", [128, 16])
        rs = A("rs", [128, 16])
        tails = [A(f"tail{i}", [128, 8, 32], BF16) for i in range(2)]
        vb = A("vb", [128, 8, 544], BF16)
        diag = A("diag", [128, 31, 128], BF16)
        ph = A("ph", [128, 4224])
        mean = A("mean", [128, 512])
        rstd = A("rstd", [128, 512])
        tmpA = A("tmpA", [128, 512])
        vcb = A("vcb", [128, 8, 512], BF16)
        halo = A("halo", [128, 32, 3])
        suf = A("suf", [128, 4, 32])
        xfm = [A(f"xfm{i}", [128, 2, 512], BF16) for i in range(2)]
        Bfm = [A(f"Bfm{i}", [128, 512], BF16) for i in range(2)]
        Cfm = [A(f"Cfm{i}", [128, 512], BF16) for i in range(2)]
        zs = [A(f"zs{i}", [128, 4, 256], BF16) for i in range(2)]
        ptmp = A("ptmp", [128, 512])
        St = A("St", [128, 2048])
        Sb = A("Sb", [128, 2048], BF16)
        dtp = A("dtp", [128, 4, 32])
        dtt = A("dtt", [128, 4, 32])
        adt = A("adt", [128, 4, 32])
        dte = A("dte", [128, 4, 32])
        cdb = A("cdb", [128, 4, 32])
        w2 = A("w2", [128, 4, 32])
        Rt = [A(f"Rt{i}", [128, 2, 4, 128], BF16) for i in range(2)]
        adthl = A("adthl", [128, 2, 4, 32], BF16)
        Ub = A("Ub", [128, 128], BF16)
        LTb = A("LTb", [128, 128], BF16)
        onesb = A("onesb", [128, 128], BF16)
        E1 = [A(f"E1{i}", [128, 512], BF16) for i in range(2)]
        E2 = [A(f"E2{i}", [128, 512], BF16) for i in range(2)]
        MT = [A(f"MT{i}", [128, 4, 128], BF16) for i in range(2)]
        CE = [A(f"CE{i}", [128, 4, 128], BF16) for i in range(2)]
        CBm = [A(f"CBm{i}", [128, 128], BF16) for i in range(2)]
        Bt = [A(f"Bt{i}", [128, 128], BF16) for i in range(2)]
        xdt = [A(f"xdt{i}", [128, 256], BF16) for i in range(2)]
        xw = [A(f"xw{i}", [128, 256], BF16) for i in range(2)]
        xtm = [A(f"xtm{i}", [128, 256]) for i in range(2)]
        yv = [A(f"yv{i}", [128, 256]) for i in range(2)]
        yn = [A(f"yn{i}", [128, 256], BF16) for i in range(2)]
        sm = [A(f"sm{i}", [128, 8]) for i in range(2)]
        nhalf = A("nhalf", [128, 4])
        ppH = A("ppH", [128, NPP])

        acc = [nc.alloc_psum_tensor(f"acc{i}", [128, 512], F32) for i in range(3)]
        ptb = nc.alloc_psum_tensor("ptb", [128, 2048], BF16)
        S0 = nc.alloc_psum_tensor("S0", [128, 512], F32)
        S1 = nc.alloc_psum_tensor("S1", [128, 512], F32)
        S2 = nc.alloc_psum_tensor("S2", [128, 512], F32)

        def slot(i):
            return slots[:, i * 512:(i + 1) * 512]

        def phk(lo, hi):
            return [('ph', b) for b in range(lo // 512, (hi - 1) // 512 + 1)]

        def vc_t(j):
            return vcb[:, j, :], [('vc', j)]

        def stg_t(i):
            return ph[:, 516 * i:516 * i + 516], [('stg', i)]

        def cv_t(i):
            return ph[:, 2064 + 512 * i:2064 + 512 * i + 512], [('cv', i)]

        def fo_t(blk):
            lo, hi = 1024 * blk, 1024 * blk + 1024
            ks = [('stg', i) for i in range(4) if 516 * i < hi and 516 * i + 516 > lo]
            ks += [('cv', i) for i in range(4) if 2064 + 512 * i < hi and 2576 + 512 * i > lo]
            return ph[:, lo:hi], ks

        state = {'acc': 0, 'wb': 0, 'xst': 0, 'gt': 0}

        def gate_tmp(allow_mean=True):
            q = state['gt']
            state['gt'] = 1 - q
            if q and allow_mean:
                return mean, 'mean'
            return tmpA, 'tmpA'


        def next_acc():
            i = state['acc']
            state['acc'] = (i + 1) % 3
            return acc[i], ('acc', i)

        wcache = {}

        def load_w(wd, ranges, kts, k0=0):
            i = state['wb']
            state['wb'] = (i + 1) % 3
            ncols = sum(n for _, n in ranges)
            assert kts * ncols <= WBN
            flat = wbuf[i][:, 0:kts * ncols]
            view = flat.rearrange("p (kt c) -> p kt c", kt=kts)
            ck = (wd.name, tuple(ranges), kts, k0)
            sub = [('wb', i, t) for t in range(4)]
            S.alias[('wb', i)] = sub
            if ck in wcache:
                S.dma('sp', flat, wcache[ck][:, :], reads=[('wsc', ck)], writes=sub, nbytes=2 * 128 * kts * ncols)
                return view, ('wb', i)
            off = 0
            part = 0
            for (c0, n) in ranges:
                for ka in range(0, kts, 8):
                    kb = min(kts, ka + 8)
                    src = wd[(k0 + ka) * 128:(k0 + kb) * 128, c0:c0 + n].rearrange("(kt p) c -> p kt c", p=128)
                    assert part < 4
                    S.dma('pool', view[:, ka:kb, off:off + n], src, writes=[sub[part]], dep_writes=sub)
                    part += 1
                off += n
            if USE_WCACHE:
                sc = nc.dram_tensor(f"wsc{len(wcache)}", [128, kts * ncols], BF16).ap()
                wcache[ck] = sc
                S.dma('sp', sc[:, :], flat, reads=[('wb', i)], writes=[('wsc', ck)], nbytes=2 * 128 * kts * ncols)
            return view, ('wb', i)

        S.dma('sp', pp[:], ppd[:, :], writes=['pp'])
        S.dma('sp', bc[:], bcd[:, :], writes=['bc'])
        S.op('pool', lambda e: e.memset(onesf[:], 1.0), writes=['onesf'])
        S.op('pool', lambda e: e.memset(nhalf[:], -0.5), writes=['nhalf'])
        S.op('dve', lambda e: e.tensor_scalar(out=ppH[:], in0=pp[:], scalar1=0.5, scalar2=None, op0=ALU.mult), reads=['pp'], writes=['ppH'])
        for tl, cmp_, nm, st, cm, bs in ((identf, ALU.is_equal, 'identf', -1, 1, 0), (U, ALU.is_gt, 'U', -1, 1, 0),
                                         (LT, ALU.is_gt, 'LT', 1, -1, 1)):
            S.op('pool', lambda e: e.affine_select(out=tl[:], in_=onesf[:], pattern=[[st, 128]],
                                                   compare_op=cmp_, fill=0.0, base=bs, channel_multiplier=cm),
                 reads=['onesf'], writes=[nm])
        S.op('dve', lambda e: e.tensor_copy(out=identb[:], in_=identf[:]), reads=['identf'], writes=['identb'])
        S.op('dve', lambda e: e.tensor_copy(out=Ub[:], in_=U[:]), reads=['U'], writes=['Ub'])
        S.op('dve', lambda e: e.tensor_copy(out=LTb[:], in_=LT[:]), reads=['LT'], writes=['LTb'])
        S.op('dve', lambda e: e.tensor_copy(out=onesb[:], in_=onesf[:]), reads=['onesf'], writes=['onesb'])
        S.op('dve', lambda e: e.memset(St[:], 0.0), writes=[('St', g) for g in range(8)])
        S.op('dve', lambda e: e.memset(Sb[:], 0.0), writes=[('Sb', g) for g in range(8)])
        S.op('dve', lambda e: e.memset(tails[1][:], 0.0), writes=[('tail', 1)])
        S.op('dve', lambda e: e.memset(vb[:], 0.0), writes=[('vb', j) for j in range(8)])
        S.op('dve', lambda e: e.memset(halo[:], 0.0), writes=[('halo', t) for t in range(32)])
        S.op('dve', lambda e: e.memset(suf[:], 0.0), writes=['suf'])
        S.op('dve', lambda e: e.memset(ph[:], 0.0), writes=[('stg', i) for i in range(4)] + [('cv', i) for i in range(4)])
        S.op('act', lambda e: e.activation(out=abc[:], in_=bc[:, B_ALOG:B_ALOG + 32], func=AF.Exp),
             reads=['bc'], writes=['abc'])
        S.op('dve', lambda e: e.tensor_scalar(out=abc[:], in0=abc[:], scalar1=-1.0, scalar2=None, op0=ALU.mult),
             reads=['abc'], writes=['abc'])

        def ppc(c):
            return pp[:, c:c + 1]

        def rms_rows(src_ap, src_keys, col):
            S.op('act', lambda e: e.activation(out=junk[:, 0:src_ap.shape[-1]], in_=src_ap, func=AF.Square,
                                               accum_out=ss[:, col:col + 1]),
                 reads=src_keys, writes=[('ss', col), 'junk'])

        def rstd_cols(c0, n, inv_n, eps, extra_reads=()):
            keys = [('ss', c) for c in range(c0, c0 + n)]
            rk = [('rs', c) for c in range(c0, c0 + n)]
            S.op('pool', lambda e: e.tensor_scalar(out=rs[:, c0:c0 + n], in0=ss[:, c0:c0 + n], scalar1=inv_n, scalar2=eps,
                                                   op0=ALU.mult, op1=ALU.add), reads=keys + list(extra_reads), writes=rk)
            S.op('pool', lambda e: e.tensor_tensor(out=rs[:, c0:c0 + n], in0=rs[:, c0:c0 + n], in1=nhalf[:, 0:n], op=ALU.pow),
                 reads=rk + ['nhalf'], writes=rk)

        def mm_fm(out_ap, out_key, wv, wkey, col0, rhs_fn, kts, ncol=128):
            for kt in range(kts):
                r_ap, r_keys = rhs_fn(kt)
                S.op('pe', lambda e: e.matmul(out_ap, lhsT=wv[:, kt, col0:col0 + ncol], rhs=r_ap,
                                              start=(kt == 0), stop=(kt == kts - 1)),
                     reads=[wkey] + r_keys, writes=[out_key])

        def hT_rhs(kt):
            return slot(kt), [('slot', kt)]

        def dump(name, ap, keys):
            if name in dbg_out:
                S.dma('sp', dbg_out[name], ap, reads=keys, writes=['dbg_' + name])

        for p in range(npre + nmain):
            main = p >= npre
            first_main = (p == npre)
            r0 = 2048 - npre * 512 + p * 512
            tail_prev = tails[(p + 1) % 2]
            tail_prev_key = ('tail', (p + 1) % 2)

            def tail_rhs(kt):
                return tail_prev[:, kt, :], [tail_prev_key]

            S.tag = ('M' if main else 'P') + 'A'
            xblk = {}

            def load_x(blk):
                i = state['xst']
                state['xst'] = (i + 1) % 2
                S.dma('sp', xst[i][:], xin[r0 + blk * 128:r0 + blk * 128 + 128, :], writes=[('xst', i)])
                return xst[i][:], [('xst', i)]
            for blk in range(4):
                src_ap, src_keys = load_x(blk)
                rms_rows(src_ap, src_keys, blk)
                rstd_cols(blk, 1, 1.0 / D, 1e-6)
                for half in range(2):
                    S.op('dve', lambda e: e.tensor_scalar(out=hb[:, half * 512:half * 512 + 512],
                                                          in0=src_ap[:, half * 512:half * 512 + 512],
                                                          scalar1=rs[:, blk:blk + 1], scalar2=None, op0=ALU.mult),
                         reads=src_keys + [('rs', blk)], writes=[('hb', half)])
                for kt in range(8):
                    S.op('pe', lambda e: e.transpose(out=ptb[:, (kt % 4) * 512 + (kt // 4) * 128:(kt % 4) * 512 + (kt // 4) * 128 + 128],
                                                     in_=hb[:, kt * 128:kt * 128 + 128], identity=identb[:]),
                         reads=[('hb', kt // 4), 'identb'], writes=[('ptb', (kt % 4) // 2)])
                for q in range(4):
                    for hh in range(2):
                        kt = q + 4 * hh
                        S.op('dve', lambda e: e.tensor_scalar(out=slot(kt)[:, blk * 128:blk * 128 + 128],
                                                              in0=ptb[:, q * 512 + hh * 128:q * 512 + hh * 128 + 128],
                                                              scalar1=ppc(P_PREG + kt), scalar2=None, op0=ALU.mult),
                             reads=[('ptb', q // 2), 'pp'], writes=[('slot', kt)])
            tcur = p % 2
            for kt in range(8):
                S.op('pool', lambda e: e.tensor_copy(out=tails[tcur][:, kt, :], in_=slot(kt)[:, 480:512]),
                     reads=[('slot', kt)], writes=[('tail', tcur)])
            if p == npre + nmain - 1:
                dump('hT', slots[:, 0:4096], [('slot', k) for k in range(8)])

            if upto < 'B':
                continue
            S.tag = 'MB'
            bw = {}

            def b1(j):
                if j % 2 == 0:
                    bw['glu'] = load_w(w_in, [(j * 128, 256), (1024 + j * 128, 256)], 8)
                wv, wk = bw['glu']
                jj = j % 2
                for (rfn, n, off) in ((tail_rhs, 32, 0), (hT_rhs, 512, 32)):
                    a_v, k_v = next_acc()
                    a_g, k_g = next_acc()
                    mm_fm(a_v[:, 0:n], k_v, wv, wk, jj * 128, rfn, 8)
                    mm_fm(a_g[:, 0:n], k_g, wv, wk, 256 + jj * 128, rfn, 8)
                    gt, gk = gate_tmp()
                    S.op('act', lambda e: e.activation(out=gt[:, 0:n], in_=a_g[:, 0:n], func=AF.Tanh,
                                                       bias=ppH[:, P_GLUB + 8 + j:P_GLUB + 9 + j], scale=0.5),
                         reads=[k_g, 'ppH'], writes=[gk])
                    S.op('dve', lambda e: e.tensor_scalar(out=gt[:, 0:n], in0=gt[:, 0:n], scalar1=0.5, scalar2=0.5,
                                                          op0=ALU.mult, op1=ALU.add), reads=[gk], writes=[gk])
                    S.op('dve', lambda e: e.scalar_tensor_tensor(out=vb[:, j, off:off + n], in0=a_v[:, 0:n],
                                                                 scalar=ppc(P_GLUB + j), in1=gt[:, 0:n],
                                                                 op0=ALU.add, op1=ALU.mult),
                         reads=[k_v, gk, 'pp'], writes=[('vb', j)])
                if first_main:
                    S.op('dve', lambda e: e.tensor_scalar(out=vb[:, j, 0:32], in0=vb[:, j, 0:32],
                                                          scalar1=ppc(P_FLAG), scalar2=None, op0=ALU.mult),
                         reads=[('vb', j), 'pp'], writes=[('vb', j)])
                S.op('dve', lambda e: e.tensor_tensor(
                    out=diag[:], in0=identb[:].unsqueeze(1).broadcast_to([128, 31, 128]),
                    in1=pp[:, P_DWW + j * 31:P_DWW + j * 31 + 31].unsqueeze(2).broadcast_to([128, 31, 128]),
                    op=ALU.mult), reads=['identb', 'pp'], writes=['diag'])
                a_c, k_c = next_acc()
                for k in range(31):
                    S.op('pe', lambda e: e.matmul(a_c[:], lhsT=diag[:, k, :], rhs=vb[:, j, 2 + k:2 + k + 512],
                                                  start=(k == 0), stop=(k == 30)),
                         reads=['diag', ('vb', j)], writes=[k_c])
                vcj, vck = vc_t(j)
                S.op('act', lambda e: e.activation(out=vcj, in_=a_c[:], func=AF.Identity,
                                                   bias=ppc(P_DWB + j), scale=1.0),
                     reads=[k_c, 'pp'], writes=vck)

            def lnfin():
                for j in range(8):
                    vcj, vck = vc_t(j)
                    S.op('act', lambda e: e.activation(out=rstd[:], in_=vcj, func=AF.Square), reads=vck, writes=['rstd'])
                    S.op('pe', lambda e: e.matmul(S0[:], lhsT=onesb[:], rhs=vcj, start=(j == 0), stop=(j == 7)),
                         reads=['onesb'] + vck, writes=['S0'])
                    S.op('pe', lambda e: e.matmul(S1[:], lhsT=onesf[:], rhs=rstd[:], start=(j == 0), stop=(j == 7)),
                         reads=['onesf', 'rstd'], writes=['S1'])
                S.op('dve', lambda e: e.tensor_scalar(out=mean[:], in0=S0[:], scalar1=1.0 / 1024, scalar2=None, op0=ALU.mult),
                     reads=['S0'], writes=['mean'])
                S.op('dve', lambda e: e.tensor_tensor(out=tmpA[:], in0=mean[:], in1=mean[:], op=ALU.mult),
                     reads=['mean'], writes=['tmpA'])
                S.op('dve', lambda e: e.scalar_tensor_tensor(out=rstd[:], in0=S1[:], scalar=1.0 / 1024, in1=tmpA[:],
                                                             op0=ALU.mult, op1=ALU.subtract),
                     reads=['S1', 'tmpA'], writes=['rstd'])
                S.op('act', lambda e: e.activation(out=rstd[:], in_=rstd[:], func=AF.Sqrt, bias=1e-5, scale=1.0),
                     reads=['rstd'], writes=['rstd'])
                S.op('dve', lambda e: e.reciprocal(out=rstd[:], in_=rstd[:]), reads=['rstd'], writes=['rstd'])

            def b2(j):
                vcj, vck = vc_t(j)
                S.op('dve', lambda e: e.tensor_tensor(out=tmpA[:], in0=vcj, in1=mean[:], op=ALU.subtract),
                     reads=vck + ['mean'], writes=['tmpA'])
                S.op('dve', lambda e: e.tensor_tensor(out=tmpA[:], in0=tmpA[:], in1=rstd[:], op=ALU.mult),
                     reads=['tmpA', 'rstd'], writes=['tmpA'])
                S.op('act', lambda e: e.activation(out=vb[:, j, 0:512], in_=tmpA[:], func=AF.Silu,
                                                   bias=ppc(P_LNB + j), scale=ppc(P_LNG + j)),
                     reads=['tmpA', 'pp'], writes=[('vb', j)])

            def b3(i):
                if i % 2 == 0:
                    bw['pw'] = load_w(w_pw, [(i * 128, 256)], 8)
                    bw['gc'] = load_w(w_in, [(OFF_GATE + i * 128, 256)], 8)
                wpv, wpk = bw['pw']
                wgv, wgk = bw['gc']
                ii = i % 2
                a_p, k_p = next_acc()
                a_g, k_g = next_acc()
                mm_fm(a_p[:], k_p, wpv, wpk, ii * 128, lambda kt: (vb[:, kt, 0:512], [('vb', kt)]), 8)
                mm_fm(a_g[:], k_g, wgv, wgk, ii * 128, hT_rhs, 8)
                gt, gk = gate_tmp()
                S.op('act', lambda e: e.activation(out=gt[:], in_=a_g[:], func=AF.Tanh,
                                                   bias=ppH[:, P_GATEB + i:P_GATEB + i + 1], scale=0.5),
                     reads=[k_g, 'ppH'], writes=[gk])
                S.op('dve', lambda e: e.tensor_scalar(out=gt[:], in0=gt[:], scalar1=0.5, scalar2=0.5,
                                                      op0=ALU.mult, op1=ALU.add), reads=[gk], writes=[gk])
                S.op('dve', lambda e: e.scalar_tensor_tensor(out=slot(8 + i), in0=a_p[:], scalar=ppc(P_PWB + i),
                                                             in1=gt[:], op0=ALU.add, op1=ALU.mult),
                     reads=[k_p, gk, 'pp'], writes=[('slot', 8 + i)])

            def bstep(g):
                if not main:
                    return
                S.tag = 'MB'
                if g < 4:
                    b1(2 * g)
                    b1(2 * g + 1)
                else:
                    if g == 4:
                        lnfin()
                        for j in range(8):
                            b2(j)
                    b3(2 * (g - 4))
                    b3(2 * (g - 4) + 1)
                S.tag = 'MC'

            if upto < 'C':
                continue
            S.tag = ('M' if main else 'P') + 'C'
            wdv, wdk = load_w(w_in, [(OFF_DT, 32)], 8)
            a_d, k_d = next_acc()
            for blk in range(4):
                for kt in range(8):
                    S.op('pe', lambda e: e.matmul(a_d[:, blk * 32:blk * 32 + 32], lhsT=slot(kt)[:, blk * 128:blk * 128 + 128],
                                                  rhs=wdv[:, kt, :], start=(kt == 0), stop=(kt == 7)),
                         reads=[('slot', kt), wdk], writes=[k_d])
            S.op('dve', lambda e: e.tensor_tensor(out=dtp[:], in0=a_d[:, 0:128].rearrange("p (b h) -> p b h", b=4),
                                                  in1=bc[:, B_DTB:B_DTB + 32].unsqueeze(1).broadcast_to([128, 4, 32]),
                                                  op=ALU.add), reads=[k_d, 'bc'], writes=['dtp'])
            S.op('act', lambda e: e.activation(out=dtp[:], in_=dtp[:], func=AF.Exp), reads=['dtp'], writes=['dtp'])
            S.op('act', lambda e: e.activation(out=dtt[:], in_=dtp[:], func=AF.Ln, bias=1.0, scale=1.0),
                 reads=['dtp'], writes=['dtt'])
            S.op('dve', lambda e: e.tensor_tensor(out=adt[:], in0=dtt[:],
                                                  in1=abc[:].unsqueeze(1).broadcast_to([128, 4, 32]), op=ALU.mult),
                 reads=['dtt', 'abc'], writes=['adt'])
            S.op('dve', lambda e: e.tensor_copy(out=adthl[:, 0], in_=adt[:]), reads=['adt'], writes=['adth'])
            S.op('dve', lambda e: e.tensor_tensor(out=adthl[:, 1], in0=adt[:], in1=adthl[:, 0], op=ALU.subtract),
                 reads=['adt', 'adth'], writes=['adtl'])
            a_e, k_e = next_acc()
            for c in range(4):
                S.op('pe', lambda e: e.matmul(a_e[:, c * 32:c * 32 + 32], lhsT=U[:], rhs=adt[:, c, :], start=True, stop=True),
                     reads=['U', 'adt'], writes=[k_e])
                S.op('pe', lambda e: e.matmul(a_e[:, 128 + c * 32:128 + c * 32 + 32], lhsT=onesf[:], rhs=adt[:, c, :],
                                              start=True, stop=True), reads=['onesf', 'adt'], writes=[k_e])
            if main:
                S.op('act', lambda e: e.activation(out=dte[:], in_=a_e[:, 0:128].rearrange("p (b h) -> p b h", b=4), func=AF.Exp),
                     reads=[k_e], writes=['dte'])
                S.op('act', lambda e: e.activation(out=cdb[:], in_=a_e[:, 128:256].rearrange("p (b h) -> p b h", b=4), func=AF.Exp),
                     reads=[k_e], writes=['cdb'])
            else:
                cdp = a_e[:, 128:256].rearrange("p (b h) -> p b h", b=4)
                S.op('dve', lambda e: e.tensor_copy(out=suf[:, 2, :], in_=cdp[:, 3, :]), reads=[k_e], writes=['suf'])
                S.op('dve', lambda e: e.tensor_tensor(out=suf[:, 1, :], in0=cdp[:, 2, :], in1=suf[:, 2, :], op=ALU.add),
                     reads=[k_e, 'suf'], writes=['suf'])
                S.op('dve', lambda e: e.tensor_tensor(out=suf[:, 0, :], in0=cdp[:, 1, :], in1=suf[:, 1, :], op=ALU.add),
                     reads=[k_e, 'suf'], writes=['suf'])
                S.op('dve', lambda e: e.tensor_tensor(out=dte[:], in0=a_e[:, 0:128].rearrange("p (b h) -> p b h", b=4),
                                                      in1=suf[:], op=ALU.add), reads=[k_e, 'suf'], writes=['dte'])
                S.op('act', lambda e: e.activation(out=dte[:], in_=dte[:], func=AF.Exp), reads=['dte'], writes=['dte'])
                S.op('dve', lambda e: e.tensor_tensor(out=cdb[:, 0, :], in0=cdp[:, 0, :], in1=suf[:, 0, :], op=ALU.add),
                     reads=[k_e, 'suf'], writes=['cdb'])
                S.op('act', lambda e: e.activation(out=cdb[:, 0, :], in_=cdb[:, 0, :], func=AF.Exp), reads=['cdb'], writes=['cdb'])
            S.op('dve', lambda e: e.tensor_tensor(out=w2[:], in0=dtt[:], in1=dte[:], op=ALU.mult),
                 reads=['dtt', 'dte'], writes=['w2'])
            if first_main:
                S.op('dve', lambda e: e.tensor_scalar(out=St[:], in0=St[:], scalar1=ppc(P_FLAG), scalar2=None, op0=ALU.mult),
                     reads=[('St', g) for g in range(8)] + ['pp'], writes=[('St', g) for g in range(8)])
                S.op('act', lambda e: e.activation(out=Sb[:], in_=St[:], func=AF.Copy),
                     reads=[('St', g) for g in range(8)], writes=[('Sb', g) for g in range(8)])

            def g1_load(g):
                ranges = [(OFF_XBC + 256 * g, 256), (OFF_XBC + 2048 + 128 * g, 128)]
                if main:
                    ranges.append((OFF_XBC + 3072 + 128 * g, 128))
                w = load_w(w_in, ranges, 8)
                wz = load_w(w_in, [(OFF_Z + 256 * g, 256)], 8) if main else None
                return w, wz

            def g1_tile(g, i, w):
                wv, wk = w
                bs = g % 2
                sg, sgk = stg_t(i)
                cvi, cvk = cv_t(i)
                tix = (2 * g + i) if i < 2 else (16 + g if i == 2 else 24 + g)
                if first_main and i == 3:
                    a_h, k_h = next_acc()
                    mm_fm(a_h[:, 0:32], k_h, wv, wk, i * 128, tail_rhs, 8)
                    S.op('dve', lambda e: e.tensor_copy(out=sg[:, 0:3], in_=a_h[:, 29:32]), reads=[k_h], writes=sgk)
                else:
                    S.op('dve', lambda e: e.tensor_copy(out=sg[:, 0:3], in_=halo[:, tix, :]), reads=[('halo', tix)], writes=sgk)
                a_m, k_m = next_acc()
                mm_fm(a_m[:], k_m, wv, wk, i * 128, hT_rhs, 8)
                S.op('act', lambda e: e.activation(out=sg[:, 3:515], in_=a_m[:], func=AF.Copy), reads=[k_m], writes=sgk)
                S.op('act', lambda e: e.activation(out=halo[:, tix, :], in_=a_m[:, 509:512], func=AF.Copy), reads=[k_m], writes=[('halo', tix)])
                if True:
                    S.op('act', lambda e: e.activation(out=cvi, in_=a_m[:], func=AF.Identity, bias=ppc(P_SCB + tix),
                                                       scale=ppc(P_SCW + tix * 4 + 3)),
                         reads=[k_m, 'pp'], writes=cvk)
                    for k in range(3):
                        S.op('dve', lambda e: e.scalar_tensor_tensor(out=cvi, in0=sg[:, k:k + 512],
                                                                     scalar=ppc(P_SCW + tix * 4 + k), in1=cvi,
                                                                     op0=ALU.mult, op1=ALU.add),
                             reads=sgk + cvk + ['pp'], writes=cvk)
                else:
                    S.op('pool', lambda e: e.tensor_scalar(out=cvi, in0=sg[:, 3:515], scalar1=ppc(P_SCW + tix * 4 + 3),
                                                           scalar2=ppc(P_SCB + tix), op0=ALU.mult, op1=ALU.add),
                         reads=sgk + ['pp'], writes=cvk)
                    for k in range(3):
                        S.op('pool', lambda e: e.tensor_scalar(out=ptmp[:], in0=sg[:, k:k + 512], scalar1=ppc(P_SCW + tix * 4 + k),
                                                               scalar2=None, op0=ALU.mult),
                             reads=sgk + ['pp'], writes=['ptmp'])
                        S.op('pool', lambda e: e.tensor_tensor(out=cvi, in0=cvi, in1=ptmp[:], op=ALU.add),
                             reads=cvk + ['ptmp'], writes=cvk)
                if i < 2:
                    S.op('act', lambda e: e.activation(out=xfm[bs][:, i, :], in_=cvi, func=AF.Silu), reads=cvk, writes=[('xfm', bs, i)])
                elif i == 2:
                    S.op('act', lambda e: e.activation(out=Bfm[bs][:], in_=cvi, func=AF.Silu), reads=cvk, writes=[('Bfm', bs)])
                else:
                    S.op('act', lambda e: e.activation(out=Cfm[bs][:], in_=cvi, func=AF.Silu), reads=cvk, writes=[('Cfm', bs)])

            def g1_z(g, half, wz):
                wzv, wzk = wz
                bs = g % 2
                a_z, k_z = next_acc()
                for bb in range(2):
                    blk = half * 2 + bb
                    for kt in range(8):
                        S.op('pe', lambda e: e.matmul(a_z[:, bb * 256:bb * 256 + 256],
                                                      lhsT=slot(kt)[:, blk * 128:blk * 128 + 128], rhs=wzv[:, kt, :],
                                                      start=(kt == 0), stop=(kt == 7)),
                             reads=[('slot', kt), wzk], writes=[k_z])
                S.op('act', lambda e: e.activation(out=zs[bs][:, half * 2:half * 2 + 2, :],
                                                   in_=a_z[:].rearrange("p (b c) -> p b c", b=2), func=AF.Silu),
                     reads=[k_z], writes=[('zs', bs, half)])

            def g1_part(g, part, w, wz):
                ntile = 4 if main else 3
                if part < ntile:
                    g1_tile(g, part, w)
                if main and part in (1, 3):
                    g1_z(g, part // 2, wz)

            def front(g, c, par):
                bs = g % 2
                cs = slice(c * 128, c * 128 + 128)
                hs = slice(4 * g, 4 * g + 4)
                for i in range(2):
                    S.op('pe', lambda e: e.transpose(out=ptb[:, 1152 + i * 128:1152 + i * 128 + 128], in_=xfm[bs][:, i, cs],
                                                     identity=identb[:]),
                         reads=[('xfm', bs, i), 'identb'], writes=[('ptb', 1)])
                S.op('pe', lambda e: e.transpose(out=ptb[:, 1024:1152], in_=Bfm[bs][:, cs], identity=identb[:]),
                     reads=[('Bfm', bs), 'identb'], writes=[('ptb', 1)])
                S.op('act', lambda e: e.activation(out=Bt[par][:], in_=ptb[:, 1024:1152], func=AF.Copy),
                     reads=[('ptb', 1)], writes=[('Bt', par)])
                x3 = ptb[:, 1152:1408].rearrange("p (h q) -> p h q", h=4)
                S.op('dve', lambda e: e.tensor_tensor(out=xw[par][:].rearrange("p (h q) -> p h q", h=4), in0=x3,
                                                      in1=w2[:, c, hs].unsqueeze(2).broadcast_to([128, 4, 64]), op=ALU.mult),
                     reads=[('ptb', 1), 'w2'], writes=[('xw', par)])
                if not main:
                    return
                S.op('dve', lambda e: e.tensor_tensor(out=xdt[par][:].rearrange("p (h q) -> p h q", h=4), in0=x3,
                                                      in1=dtt[:, c, hs].unsqueeze(2).broadcast_to([128, 4, 64]), op=ALU.mult),
                     reads=[('ptb', 1), 'dtt'], writes=[('xdt', par)])
                S.op('act', lambda e: e.activation(out=xtm[par][:], in_=ptb[:, 1152:1408], func=AF.Copy),
                     reads=[('ptb', 1)], writes=[('xtm', par)])
                S.op('pe', lambda e: e.matmul(S2[:, 0:128], lhsT=Bfm[bs][:, cs], rhs=Cfm[bs][:, cs], start=True, stop=True),
                     reads=[('Bfm', bs), ('Cfm', bs)], writes=['S2'])
                S.op('dve', lambda e: e.tensor_tensor(out=CBm[par][:], in0=S2[:, 0:128], in1=LT[:], op=ALU.mult),
                     reads=['S2', 'LT'], writes=[('CBm', par)])
                S.op('dve', lambda e: e.tensor_tensor(out=Rt[par][:], in0=adthl[:, :, c, hs].unsqueeze(3).broadcast_to([128, 2, 4, 128]),
                                                      in1=LTb[:].unsqueeze(1).unsqueeze(1).broadcast_to([128, 2, 4, 128]), op=ALU.mult),
                     reads=['adth', 'adtl', 'LTb'], writes=[('Rt', par)])
                for t in range(2):
                    Rf = Rt[par][:, t].rearrange("p h l -> p (h l)")
                    S.op('pe', lambda e: e.matmul(S0[:], lhsT=Ub[:], rhs=Rf, start=(t == 0), stop=(t == 1)),
                         reads=['Ub', ('Rt', par)], writes=['S0'])
                for t in range(2):
                    Rf = Rt[par][:, t].rearrange("p h l -> p (h l)")
                    S.op('pe', lambda e: e.matmul(S1[:], lhsT=onesb[:], rhs=Rf, start=(t == 0), stop=(t == 1)),
                         reads=['onesb', ('Rt', par)], writes=['S1'])
                S.op('act', lambda e: e.activation(out=E1[par][:], in_=S0[:], func=AF.Exp), reads=['S0'], writes=[('E1', par)])
                S.op('act', lambda e: e.activation(out=E2[par][:], in_=S1[:], func=AF.Exp), reads=['S1'], writes=[('E2', par)])
                S.op('dve', lambda e: e.tensor_tensor(out=MT[par][:], in0=E1[par][:].rearrange("p (h l) -> p h l", h=4),
                                                      in1=CBm[par][:].unsqueeze(1).broadcast_to([128, 4, 128]), op=ALU.mult),
                     reads=[('E1', par), ('CBm', par)], writes=[('MT', par)])
                S.op('dve', lambda e: e.tensor_tensor(out=CE[par][:], in0=E2[par][:].rearrange("p (h l) -> p h l", h=4),
                                                      in1=Cfm[bs][:, cs].unsqueeze(1).broadcast_to([128, 4, 128]), op=ALU.mult),
                     reads=[('E2', par), ('Cfm', bs)], writes=[('CE', par)])

            def back(g, c, par):
                bs = g % 2
                cs = slice(c * 128, c * 128 + 128)
                hs = slice(4 * g, 4 * g + 4)
                if main:
                    a_y, k_y = next_acc()
                    for h in range(4):
                        S.op('pe', lambda e: e.matmul(a_y[:, h * 64:h * 64 + 64], lhsT=MT[par][:, h, :], rhs=xdt[par][:, h * 64:h * 64 + 64],
                                                      start=True, stop=False),
                             reads=[('MT', par), ('xdt', par)], writes=[k_y])
                        S.op('pe', lambda e: e.matmul(a_y[:, h * 64:h * 64 + 64], lhsT=CE[par][:, h, :],
                                                      rhs=Sb[:, (4 * g + h) * 64:(4 * g + h) * 64 + 64], start=False, stop=True),
                             reads=[('CE', par), ('Sb', g)], writes=[k_y])
                    yvp, ynp, smp = yv[par], yn[par], sm[par]
                    S.op('dve', lambda e: e.tensor_tensor(out=yvp[:].rearrange("p (h q) -> p h q", h=4),
                                                          in0=xtm[par][:].rearrange("p (h q) -> p h q", h=4),
                                                          in1=bc[:, B_DSK + 4 * g:B_DSK + 4 * g + 4].unsqueeze(2).broadcast_to([128, 4, 64]),
                                                          op=ALU.mult), reads=[('xtm', par), 'bc'], writes=[('yv', par)])
                    S.op('dve', lambda e: e.tensor_tensor(out=yvp[:], in0=a_y[:, 0:256], in1=yvp[:], op=ALU.add),
                         reads=[k_y, ('yv', par)], writes=[('yv', par)])
                    S.op('dve', lambda e: e.tensor_tensor(out=yvp[:], in0=yvp[:], in1=zs[bs][:, c, :], op=ALU.mult),
                         reads=[('yv', par), ('zs', bs, c // 2)], writes=[('yv', par)])
                    S.op('act', lambda e: e.activation(out=ynp[:], in_=yvp[:], func=AF.Square, accum_out=smp[:, 0:1]),
                         reads=[('yv', par)], writes=[('sm', par), ('yn', par)])
                    S.op('pool', lambda e: e.tensor_scalar(out=smp[:, 1:2], in0=smp[:, 0:1], scalar1=1.0 / 256, scalar2=1e-6,
                                                           op0=ALU.mult, op1=ALU.add), reads=[('sm', par)], writes=[('sm1', par)])
                    S.op('pool', lambda e: e.tensor_tensor(out=smp[:, 2:3], in0=smp[:, 1:2], in1=nhalf[:, 0:1], op=ALU.pow),
                         reads=[('sm1', par), 'nhalf'], writes=[('sm2', par)])
                    S.op('dve', lambda e: e.tensor_scalar(out=ynp[:], in0=yvp[:], scalar1=smp[:, 2:3], scalar2=None, op0=ALU.mult),
                         reads=[('yv', par), ('sm2', par)], writes=[('yn', par)])
                    for i in range(2):
                        S.op('pe', lambda e: e.transpose(out=ptb[:, 512 + i * 128:512 + i * 128 + 128], in_=ynp[:, i * 128:i * 128 + 128],
                                                         identity=identb[:]),
                             reads=[('yn', par), 'identb'], writes=[('ptb', 0)])
                    for i in range(2):
                        S.op('act', lambda e: e.activation(out=slot(16 + 2 * g + i)[:, cs], in_=ptb[:, 512 + i * 128:512 + i * 128 + 128],
                                                           func=AF.Copy, scale=ppc(P_NG + 2 * g + i)),
                             reads=[('ptb', 0), 'pp'], writes=[('slot', 16 + 2 * g + i)])
                a_s, k_s = S2[:, 128:384], 'S2'
                Sg = St[:, 256 * g:256 * g + 256]
                if main:
                    S.op('pe', lambda e: e.matmul(a_s[:, 0:256], lhsT=Bt[par][:], rhs=xw[par][:], start=True, stop=True),
                         reads=[('Bt', par), ('xw', par)], writes=[k_s])
                    cdv = cdb[:, c, hs]
                else:
                    S.op('pe', lambda e: e.matmul(a_s[:, 0:256], lhsT=Bt[par][:], rhs=xw[par][:], start=(c == 0), stop=(c == 3)),
                         reads=[('Bt', par), ('xw', par)], writes=[k_s])
                    if c < 3:
                        return
                    cdv = cdb[:, 0, hs]
                S.op('dve', lambda e: e.tensor_tensor(out=Sg.rearrange("p (h q) -> p h q", h=4),
                                                      in0=Sg.rearrange("p (h q) -> p h q", h=4),
                                                      in1=cdv.unsqueeze(2).broadcast_to([128, 4, 64]), op=ALU.mult),
                     reads=[('St', g), 'cdb'], writes=[('St', g)])
                S.op('dve', lambda e: e.tensor_tensor(out=Sg, in0=a_s[:, 0:256], in1=Sg, op=ALU.add),
                     reads=[k_s, ('St', g)], writes=[('St', g)])
                if main:
                    S.op('act', lambda e: e.activation(out=Sb[:, 256 * g:256 * g + 256], in_=Sg, func=AF.Copy),
                         reads=[('St', g)], writes=[('Sb', g)])

            wcur = g1_load(0)
            for part in range(4):
                g1_part(0, part, *wcur)
            for g in range(8):
                bstep(g)
                if g < 7:
                    wnext = g1_load(g + 1)
                front(g, 0, 0)
                for c in range(4):
                    if c < 3:
                        front(g, c + 1, (c + 1) % 2)
                    back(g, c, c % 2)
                    if g < 7:
                        g1_part(g + 1, c, *wnext)

            if not main:
                continue

            if upto < 'D':
                continue
            S.tag = 'MD'
            for i in range(8):
                if i % 2 == 0:
                    wsv, wsk = load_w(w_so, [(i * 128, 256)], 16)
                if i % 4 == 0:
                    wgv, wgk = load_w(w_in, [(OFF_GATE + 1024 + i * 128, 512)], 8)
                a_p, k_p = next_acc()
                a_g, k_g = next_acc()
                mm_fm(a_p[:], k_p, wsv, wsk, (i % 2) * 128, lambda kt: (slot(16 + kt), [('slot', 16 + kt)]), 16)
                mm_fm(a_g[:], k_g, wgv, wgk, (i % 4) * 128, hT_rhs, 8)
                gt, gk = gate_tmp()
                S.op('act', lambda e: e.activation(out=gt[:], in_=a_g[:], func=AF.Tanh,
                                                   bias=ppH[:, P_GATEB + 8 + i:P_GATEB + 9 + i], scale=0.5),
                     reads=[k_g, 'ppH'], writes=[gk])
                S.op('dve', lambda e: e.tensor_scalar(out=gt[:], in0=gt[:], scalar1=0.5, scalar2=0.5,
                                                      op0=ALU.mult, op1=ALU.add), reads=[gk], writes=[gk])
                S.op('dve', lambda e: e.tensor_tensor(out=gt[:], in0=a_p[:], in1=gt[:], op=ALU.mult),
                     reads=[k_p, gk], writes=[gk])
                S.op('dve', lambda e: e.tensor_tensor(out=slot(8 + i), in0=gt[:], in1=slot(8 + i), op=ALU.add),
                     reads=[gk, ('slot', 8 + i)], writes=[('slot', 8 + i)])

            for half in range(2):
                wov, wok = load_w(w_o, [(half * 512, 512)], 8)
                for blk in range(4):
                    a_o, k_o = next_acc()
                    for kt in range(8):
                        S.op('pe', lambda e: e.matmul(a_o[:], lhsT=slot(8 + kt)[:, blk * 128:blk * 128 + 128], rhs=wov[:, kt, :],
                                                      start=(kt == 0), stop=(kt == 7)),
                             reads=[('slot', 8 + kt), wok], writes=[k_o])
                    S.op('act', lambda e: e.activation(out=x1[:, blk, half * 512:half * 512 + 512], in_=a_o[:], func=AF.Copy),
                         reads=[k_o], writes=[('x1', blk)])
                    S.op('act', lambda e: e.activation(out=junk[:, 0:512], in_=a_o[:], func=AF.Square,
                                                       accum_out=ss[:, 4 + blk * 2 + half:5 + blk * 2 + half]),
                         reads=[k_o], writes=[('ss', 4 + blk * 2 + half), 'junk'])
            for blk in range(4):
                S.op('dve', lambda e: e.tensor_tensor(out=ss[:, 12 + blk:13 + blk], in0=ss[:, 4 + 2 * blk:5 + 2 * blk],
                                                      in1=ss[:, 5 + 2 * blk:6 + 2 * blk], op=ALU.add),
                     reads=[('ss', 4 + 2 * blk), ('ss', 5 + 2 * blk)], writes=[('ss', 12 + blk)])
            rstd_cols(12, 4, 1.0 / D, 1e-6)
            for blk in range(4):
                i = state['xst']
                state['xst'] = (i + 1) % 2
                S.dma('sp', xst[i][:], xin[r0 + blk * 128:r0 + blk * 128 + 128, :], writes=[('xst', i)])
                S.op('dve', lambda e: e.scalar_tensor_tensor(out=x1[:, blk, :], in0=x1[:, blk, :], scalar=rs[:, 12 + blk:13 + blk],
                                                             in1=bc[:, B_POSTG:B_POSTG + D], op0=ALU.mult, op1=ALU.mult),
                     reads=[('x1', blk), ('rs', 12 + blk), 'bc'], writes=[('x1', blk)])
                S.op('dve', lambda e: e.tensor_tensor(out=x1[:, blk, :], in0=x1[:, blk, :], in1=xst[i][:], op=ALU.add),
                     reads=[('x1', blk), ('xst', i)], writes=[('x1', blk)])

            if upto < 'E':
                continue
            S.tag = 'ME'
            for blk in range(4):
                src_ap, src_keys = x1[:, blk, :], [('x1', blk)]
                rms_rows(src_ap, src_keys, blk)
                rstd_cols(blk, 1, 1.0 / D, 1e-6)
                for half in range(2):
                    S.op('dve', lambda e: e.tensor_scalar(out=hb[:, half * 512:half * 512 + 512],
                                                          in0=src_ap[:, half * 512:half * 512 + 512],
                                                          scalar1=rs[:, blk:blk + 1], scalar2=None, op0=ALU.mult),
                         reads=src_keys + [('rs', blk)], writes=[('hb', half)])
                for kt in range(8):
                    S.op('pe', lambda e: e.transpose(out=ptb[:, (kt % 4) * 512 + (kt // 4) * 128:(kt % 4) * 512 + (kt // 4) * 128 + 128],
                                                     in_=hb[:, kt * 128:kt * 128 + 128], identity=identb[:]),
                         reads=[('hb', kt // 4), 'identb'], writes=[('ptb', (kt % 4) // 2)])
                for q in range(4):
                    for hh in range(2):
                        kt = q + 4 * hh
                        S.op('dve', lambda e: e.tensor_scalar(out=slot(kt)[:, blk * 128:blk * 128 + 128],
                                                              in0=ptb[:, q * 512 + hh * 128:q * 512 + hh * 128 + 128],
                                                              scalar1=ppc(P_FPREG + kt), scalar2=None, op0=ALU.mult),
                             reads=[('ptb', q // 2), 'pp'], writes=[('slot', kt)])
            for i in range(22):
                if i % 2 == 0:
                    wv, wk = load_w(w_gu, [(i * 128, 256), (FH + i * 128, 256)], 8)
                ii = i % 2
                a_g, k_g = next_acc()
                a_u, k_u = next_acc()
                mm_fm(a_g[:], k_g, wv, wk, ii * 128, hT_rhs, 8)
                mm_fm(a_u[:], k_u, wv, wk, 256 + ii * 128, hT_rhs, 8)
                gt, gk = gate_tmp()
                S.op('act', lambda e: e.activation(out=gt[:], in_=a_g[:], func=AF.Silu), reads=[k_g], writes=[gk])
                S.op('dve', lambda e: e.tensor_tensor(out=slot(8 + i), in0=a_u[:], in1=gt[:], op=ALU.mult),
                     reads=[k_u, gk], writes=[('slot', 8 + i)])
            for q in range(4):
                wv, wk = load_w(w_dn, [(q * 256, 256)], 22)
                for half in range(2):
                    a_o, k_o = next_acc()
                    for bb in range(2):
                        blk = half * 2 + bb
                        for kt in range(22):
                            S.op('pe', lambda e: e.matmul(a_o[:, bb * 256:bb * 256 + 256], lhsT=slot(8 + kt)[:, blk * 128:blk * 128 + 128],
                                                          rhs=wv[:, kt, :], start=(kt == 0), stop=(kt == 21)),
                                 reads=[('slot', 8 + kt), wk], writes=[k_o])
                    for bb in range(2):
                        blk = half * 2 + bb
                        fob, fok = fo_t(blk)
                        S.op('act', lambda e: e.activation(out=fob[:, q * 256:q * 256 + 256], in_=a_o[:, bb * 256:bb * 256 + 256], func=AF.Copy),
                             reads=[k_o], writes=fok)
            for blk in range(4):
                fob, fok = fo_t(blk)
                rms_rows(fob, fok, 8 + blk)
            rstd_cols(8, 4, 1.0 / D, 1e-6)
            for blk in range(4):
                fob, fok = fo_t(blk)
                S.op('dve', lambda e: e.scalar_tensor_tensor(out=fob, in0=fob, scalar=rs[:, 8 + blk:9 + blk],
                                                             in1=bc[:, B_FPOSTG:B_FPOSTG + D], op0=ALU.mult, op1=ALU.mult),
                     reads=fok + [('rs', 8 + blk), 'bc'], writes=fok)
                S.op('dve', lambda e: e.tensor_tensor(out=fob, in0=fob, in1=x1[:, blk, :], op=ALU.add),
                     reads=fok + [('x1', blk)], writes=fok)
                ro = (p - npre) * 512 + blk * 128
                S.dma('sp', outd[ro:ro + 128, :], fob, reads=fok, writes=[('out', p, blk)])
        S.finish('sp')
    return nc


def _prep(inputs):
    f = lambda k: np.ascontiguousarray(np.asarray(inputs[k], dtype=np.float32)[0])
    pp = np.zeros((128, NPP), np.float32)

    def pcol(v, c0):
        n = v.shape[0] // 128
        pp[:, c0:c0 + n] = v.reshape(n, 128).T
    pcol(f("gate_b"), P_GATEB)
    pcol(f("glu_b"), P_GLUB)
    dww = f("conv_dw_w")
    pp[:, P_DWW:P_DWW + 248] = dww.T.reshape(8, 128, 31).transpose(1, 0, 2).reshape(128, 248)
    pcol(f("conv_dw_b"), P_DWB)
    pcol(f("conv_ln_g"), P_LNG)
    pcol(f("conv_ln_b"), P_LNB)
    pcol(f("conv_pw_b"), P_PWB)
    scw = f("ssm_conv_w")
    pp[:, P_SCW:P_SCW + 128] = scw.T.reshape(32, 128, 4).transpose(1, 0, 2).reshape(128, 128)
    pcol(f("ssm_conv_b"), P_SCB)
    pcol(f("ssm_norm_g"), P_NG)
    pcol(f("mix_pre_g"), P_PREG)
    pcol(f("ffn_pre_g"), P_FPREG)
    bc = np.zeros((128, NBC), np.float32)
    bc[:, B_POSTG:B_POSTG + D] = f("mix_post_g")[None, :]
    bc[:, B_FPOSTG:B_FPOSTG + D] = f("ffn_post_g")[None, :]
    bc[:, B_DTB:B_DTB + 32] = f("dt_bias")[None, :]
    bc[:, B_ALOG:B_ALOG + 32] = f("a_log")[None, :]
    bc[:, B_DSK:B_DSK + 32] = f("d_skip")[None, :]
    shared = {"w_in": f("w_in"), "w_pw": f("conv_pw_w"), "w_so": f("ssm_out_w"), "w_o": f("w_out"),
              "w_gu": f("w_gate_up"), "w_dn": f("w_down"), "bc": bc}
    x = np.asarray(inputs["x"], dtype=np.float32)
    in_maps = []
    for core in range(8):
        b, hf = core // 2, core % 2
        xin = np.zeros((4096, D), np.float32)
        if hf == 1:
            xin[:] = x[b]
        else:
            xin[2048:] = x[b, :2048]
        ppc = pp.copy()
        ppc[:, P_FLAG] = float(hf)
        m = dict(shared)
        m["xin"] = xin
        m["pp"] = ppc
        in_maps.append(m)
    return in_maps


def kernel(**inputs):
    in_maps = _prep(inputs)
    nc = build_program()
    res = run_bass_kernel_spmd(nc, in_maps, core_ids=list(range(8)))
    out = np.zeros((4, 4096, D), np.float32)
    for core in range(8):
        b, hf = core // 2, core % 2
        out[b, hf * 2048:(hf + 1) * 2048] = res.results[core]["out"]
    return out
```

```python
import numpy as np
from contextlib import ExitStack
import concourse.bass as bass
import concourse.mybir as mybir
from concourse.bass_utils import run_bass_kernel_spmd

F32 = mybir.dt.float32
BF16 = mybir.dt.bfloat16
AF = mybir.ActivationFunctionType
ALU = mybir.AluOpType

D = 1024
T = 512
OFF_Z = 2048
OFF_XBC = 4096
OFF_DT = 8192
OFF_GATE = 8224
IN_COLS = 10272
FH = 2816
P_GATEB, P_GLUB, P_DWW, P_DWB, P_LNG, P_LNB, P_PWB = 0, 16, 32, 280, 288, 296, 304
P_SCW, P_SCB, P_NG, P_FLAG, P_PREG, P_FPREG, NPP = 312, 440, 472, 488, 489, 497, 512
B_POSTG, B_FPOSTG, B_DTB, B_ALOG, B_DSK, NBC = 0, 1024, 2048, 2080, 2112, 2144
NPRE = 4
NMAIN = 4
CUT = 99
LAST_SIM = None
DEBUG_SCHED = False
DMA_BW = 300.0
SCHED_WIN = 48
USE_WCACHE = True
PRIO_MODE = 2
PRIO_ALPHA = 0.05


class _Call:
    def __init__(self, name, a, kw):
        self.name, self.a, self.kw = name, a, kw


class _Rec:
    def __getattr__(self, name):
        return lambda *a, **kw: _Call(name, a, kw)


def _free(ap):
    n = 1
    for d in ap.shape[1:]:
        n *= d
    return n


class Sched:
    EXCL = set([('acc', 0), ('acc', 1), ('acc', 2), ('ptb', 0), ('ptb', 1), 'S0', 'S1', 'S2'])

    def __init__(self, nc, es, ndma=8):
        self.nc = nc
        self.eng = {'pe': nc.tensor, 'act': nc.scalar, 'dve': nc.vector,
                    'pool': nc.gpsimd, 'sp': nc.sync}
        self.sem = {k: es.enter_context(nc.semaphore('s_' + k)) for k in self.eng}
        self.dsem = {}
        for q in ('sp', 'pool'):
            self.dsem[q] = [es.enter_context(nc.semaphore(f'd_{q}{i}')) for i in range(ndma)]
        self.ops = []
        self.lastw = {}
        self.readers = {}
        self.rec = _Rec()
        self.tag = 'init'

    def _mkdeps(self, eng, reads, writes):
        deps = set()
        why = {}
        for r in reads:
            w = self.lastw.get(r)
            if w is not None:
                deps.add(w)
                why[w] = ('RAW', r)
            if r in self.EXCL:
                for i in self.readers.get(r, ()):
                    if self.ops[i]['eng'] != eng:
                        deps.add(i)
                        why[i] = ('XRD', r)
        for w_ in writes:
            w = self.lastw.get(w_)
            if w is not None:
                deps.add(w)
                why.setdefault(w, ('WAW', w_))
            for i in self.readers.get(w_, ()):
                deps.add(i)
                why.setdefault(i, ('WAR', w_))
        self._why = why
        return deps

    def _note(self, idx, reads, writes):
        for r in reads:
            self.readers.setdefault(r, []).append(idx)
        for w in writes:
            self.lastw[w] = idx
            self.readers[w] = []

    def _est(self, eng, call):
        kw = call.kw
        if eng == 'pe':
            if call.name == 'transpose':
                return 110.0
            rhs = kw.get('rhs')
            n = _free(rhs)
            t = 25.0 + n / 2.0
            if rhs.dtype == F32:
                t *= 4
            return max(t, 35.0)
        out = kw.get('out') if 'out' in kw else (call.a[0] if call.a else None)
        n = _free(out) if out is not None else 64
        if eng == 'act':
            return 230.0 + n / 1.2
        if eng == 'dve':
            return 120.0 + n / 0.96
        return 600.0 + n * 7.0

    def op(self, eng, fn, reads=(), writes=()):
        call = fn(self.rec)
        idx = len(self.ops)
        deps = self._mkdeps(eng, reads, writes)
        t = self._est(eng, call)
        tset = None
        if eng == 'act':
            f = call.kw.get('func')
            tset = {AF.Exp: 'exp', AF.Ln: 'ln', AF.Silu: 'silu', AF.Tanh: 'silu', AF.Sigmoid: 'sig', AF.Sqrt: 'sqrt'}.get(f)
        self.ops.append(dict(eng=eng, call=call, deps=deps, occ=t, lat=t + 60.0, dma=False, tag=self.tag, tset=tset, why=self._why))
        self._note(idx, reads, writes)

    def dma(self, q, out, in_, reads=(), writes=(), nbytes=None):
        idx = len(self.ops)
        deps = self._mkdeps(q, reads, writes)
        if nbytes is None:
            nbytes = 4 * 128 * _free(out) if out.shape[0] == 128 else 4 * out.shape[0] * _free(out)
        self.ops.append(dict(eng=q, call=_Call('dma_start', (), dict(out=out, in_=in_)), deps=deps,
                             occ=(900.0 if q == 'pool' else 100.0), lat=2000.0 + nbytes / DMA_BW, dma=True, nbytes=nbytes, tag=self.tag, why=self._why))
        self._note(idx, reads, writes)

    def finish(self, eng='sp', reorder=True):
        ops = self.ops
        n = len(ops)
        succ = [[] for _ in range(n)]
        ndep = [0] * n
        for i, o in enumerate(ops):
            ndep[i] = len(o['deps'])
            for d in o['deps']:
                succ[d].append(i)
        ready = {k: [] for k in self.eng}
        for i in range(n):
            if ndep[i] == 0:
                ready[ops[i]['eng']].append(i)
        blev = [0.0] * n
        for i in range(n - 1, -1, -1):
            m = 0.0
            for s_ in succ[i]:
                if blev[s_] > m:
                    m = blev[s_]
            blev[i] = m + ops[i]['lat']
        free_t = {k: 0.0 for k in self.eng}
        fin = [0.0] * n
        order = []
        dma_pipe = 0.0
        WIN = SCHED_WIN
        last_on = {}
        cur_set = None
        while len(order) < n:
            best = None
            for k, lst in ready.items():
                if not lst:
                    continue
                lst.sort()
                cand = None
                for i in (lst[:WIN] if reorder else lst[:1]):
                    dr = 0.0
                    for d in ops[i]['deps']:
                        if k == 'pe' and ops[d]['eng'] == 'pe':
                            continue
                        f = fin[d] + (0.0 if ops[d]['eng'] == k else 150.0)
                        if f > dr:
                            dr = f
                    st = max(free_t[k], dr)
                    ts_ = ops[i].get('tset')
                    if ts_ is not None and ts_ != cur_set:
                        st += 1400.0
                    if PRIO_MODE == 1:
                        key_ = (st, -blev[i], i)
                    elif PRIO_MODE == 2:
                        key_ = (st - PRIO_ALPHA * blev[i], i)
                    else:
                        key_ = (st, i)
                    if cand is None or key_ < cand[2]:
                        cand = (st, i, key_)
                if best is None or cand[:2] < best[0][:2]:
                    best = (cand, k)
            (st, i, _k), k = best
            if not reorder:
                st = 0.0
                for d in ops[i]['deps']:
                    st = max(st, fin[d])
                st = max(st, free_t[k])
            o = ops[i]
            ready[k].remove(i)
            if o.get('tset') is not None:
                cur_set = o['tset']
            if DEBUG_SCHED:
                bd, bt = None, -1.0
                for d in o['deps']:
                    f = fin[d] + (0.0 if ops[d]['eng'] == k else 150.0)
                    if f > bt:
                        bd, bt = d, f
                if free_t[k] >= bt:
                    o['bind'] = ('eng', last_on.get(k))
                else:
                    o['bind'] = ('dep', bd)
                o['st'] = st
                last_on[k] = i
            free_t[k] = st + o['occ']
            if o['dma']:
                done = max(st + 2000.0, dma_pipe) + o["nbytes"] / DMA_BW
                dma_pipe = done
                fin[i] = done
            else:
                fin[i] = st + o['lat']
            order.append(i)
            for s_ in succ[i]:
                ndep[s_] -= 1
                if ndep[s_] == 0:
                    ready[ops[s_]['eng']].append(s_)
        self.sim_end = max(fin) if fin else 0.0
        global LAST_SIM
        LAST_SIM = dict(ops=ops if DEBUG_SCHED else None, fin=fin, end=self.sim_end, busy={k: sum(o['occ'] for o in ops if o['eng'] == k) for k in self.eng})
        cnt = {k: 0 for k in self.eng}
        dcnt = {(q, i): 0 for q in self.dsem for i in range(len(self.dsem[q]))}
        drr = {q: 0 for q in self.dsem}
        known = {k: {} for k in self.eng}
        ev = [None] * n

        def semof(key):
            return self.sem[key] if isinstance(key, str) else self.dsem[key[0]][key[1]]
        for i in order:
            o = ops[i]
            k = o['eng']
            e = self.eng[k]
            need = {}
            if o['dma']:
                slot_i = drr[k]
                drr[k] = (slot_i + 1) % len(self.dsem[k])
                dkey = (k, slot_i)
                if dcnt[dkey] > 0:
                    need[dkey] = dcnt[dkey]
            for d in o['deps']:
                sk, c = ev[d]
                if sk == 'pe' and k == 'pe':
                    continue
                if need.get(sk, 0) < c:
                    need[sk] = c
            for sk, c in need.items():
                if known[k].get(sk, 0) >= c:
                    continue
                e.wait_ge(semof(sk), c * (1 if isinstance(sk, str) else 16))
                known[k][sk] = c
            inst = getattr(e, o['call'].name)(*o['call'].a, **o['call'].kw)
            if o['dma']:
                inst.then_inc(self.dsem[k][slot_i], 16)
                dcnt[dkey] += 1
                ev[i] = (dkey, dcnt[dkey])
            else:
                cnt[k] += 1
                inst.then_inc(self.sem[k], 1)
                ev[i] = (k, cnt[k])
        for k in self.eng:
            if cnt[k] and k != eng:
                self.eng[eng].wait_ge(self.sem[k], cnt[k])
        for dkey, c in dcnt.items():
            if c:
                self.eng[eng].wait_ge(self.dsem[dkey[0]][dkey[1]], c * 16)


def build_program(npre=NPRE, nmain=NMAIN, dbg=None, upto='E'):
    nc = bass.Bass("TRN2", target_bir_lowering=False)

    def din(name, shape):
        return nc.dram_tensor(name, shape, F32, kind="ExternalInput").ap()
    xin = din("xin", [4096, D])
    w_in = din("w_in", [D, IN_COLS])
    w_pw = din("w_pw", [D, D])
    w_so = din("w_so", [2048, D])
    w_o = din("w_o", [D, D])
    w_gu = din("w_gu", [D, 2 * FH])
    w_dn = din("w_dn", [FH, D])
    ppd = din("pp", [128, NPP])
    bcd = din("bc", [128, NBC])
    outd = nc.dram_tensor("out", [2048, D], F32, kind="ExternalOutput").ap()
    dbg_out = {}
    if dbg:
        for name, shape in dbg.items():
            dbg_out[name] = nc.dram_tensor("dbg_" + name, shape, F32, kind="ExternalOutput").ap()

    es = ExitStack()
    with es:
        S = Sched(nc, es)

        def A(name, shape, dt=F32):
            return nc.alloc_sbuf_tensor("sb_" + name, shape, dt)
        identf = A("identf", [128, 128])
        identb = A("identb", [128, 128], BF16)
        U = A("U", [128, 128])
        LT = A("LT", [128, 128])
        onesf = A("onesf", [128, 128])
        pp = A("pp", [128, NPP])
        bc = A("bc", [128, NBC])
        abc = A("abc", [128, 32])
        slots = A("slots", [128, 32 * 512], BF16)
        WBN = 5632
        wbuf = [A(f"wb{i}", [128, WBN], BF16) for i in range(3)]
        xst = [A(f"xst{i}", [128, D]) for i in range(2)]
        x1 = A("x1", [128, 4, D])
        hb = A("hb", [128, D], BF16)
        junk = A("junk", [128, D], BF16)
        ss = A("ss", [128, 16])
        rs = A("rs", [128, 16])
        tails = [A(f"tail{i}", [128, 8, 32], BF16) for i in range(2)]
        vb = A("vb", [128, 8, 544], BF16)
        diag = A("diag", [128, 31, 128], BF16)
        ph = A("ph", [128, 4224])
        mean = A("mean", [128, 512])
        rstd = A("rstd", [128, 512])
        tmpA = A("tmpA", [128, 512])
        vcb = A("vcb", [128, 8, 512], BF16)
        halo = A("halo", [128, 32, 3])
        suf = A("suf", [128, 4, 32])
        xfm = [A(f"xfm{i}", [128, 2, 512], BF16) for i in range(2)]
        Bfm = [A(f"Bfm{i}", [128, 512], BF16) for i in range(2)]
        Cfm = [A(f"Cfm{i}", [128, 512], BF16) for i in range(2)]
        zs = [A(f"zs{i}", [128, 4, 256], BF16) for i in range(2)]
        ptmp = A("ptmp", [128, 512])
        St = A("St", [128, 2048])
        Sb = A("Sb", [128, 2048], BF16)
        dtp = A("dtp", [128, 4, 32])
        dtt = A("dtt", [128, 4, 32])
        adt = A("adt", [128, 4, 32])
        dte = A("dte", [128, 4, 32])
        cdb = A("cdb", [128, 4, 32])
        w2 = A("w2", [128, 4, 32])
        Rt = [A(f"Rt{i}", [128, 2, 4, 128], BF16) for i in range(2)]
        adthl = A("adthl", [128, 2, 4, 32], BF16)
        Ub = A("Ub", [128, 128], BF16)
        LTb = A("LTb", [128, 128], BF16)
        onesb = A("onesb", [128, 128], BF16)
        E1 = [A(f"E1{i}", [128, 512], BF16) for i in range(2)]
        E2 = [A(f"E2{i}", [128, 512], BF16) for i in range(2)]
        MT = [A(f"MT{i}", [128, 4, 128], BF16) for i in range(2)]
        CE = [A(f"CE{i}", [128, 4, 128], BF16) for i in range(2)]
        CBm = [A(f"CBm{i}", [128, 128], BF16) for i in range(2)]
        Bt = [A(f"Bt{i}", [128, 128], BF16) for i in range(2)]
        xdt = [A(f"xdt{i}", [128, 256], BF16) for i in range(2)]
        xw = [A(f"xw{i}", [128, 256], BF16) for i in range(2)]
        xtm = [A(f"xtm{i}", [128, 256]) for i in range(2)]
        yv = [A(f"yv{i}", [128, 256]) for i in range(2)]
        yn = [A(f"yn{i}", [128, 256], BF16) for i in range(2)]
        sm = [A(f"sm{i}", [128, 8]) for i in range(2)]
        nhalf = A("nhalf", [128, 4])
        ppH = A("ppH", [128, NPP])

        acc = [nc.alloc_psum_tensor(f"acc{i}", [128, 512], F32) for i in range(3)]
        ptb = nc.alloc_psum_tensor("ptb", [128, 2048], BF16)
        S0 = nc.alloc_psum_tensor("S0", [128, 512], F32)
        S1 = nc.alloc_psum_tensor("S1", [128, 512], F32)
        S2 = nc.alloc_psum_tensor("S2", [128, 512], F32)

        def slot(i):
            return slots[:, i * 512:(i + 1) * 512]

        def phk(lo, hi):
            return [('ph', b) for b in range(lo // 512, (hi - 1) // 512 + 1)]

        def vc_t(j):
            return vcb[:, j, :], [('vc', j)]

        def stg_t(i):
            return ph[:, 516 * i:516 * i + 516], [('stg', i)]

        def cv_t(i):
            return ph[:, 2064 + 512 * i:2064 + 512 * i + 512], [('cv', i)]

        def fo_t(blk):
            lo, hi = 1024 * blk, 1024 * blk + 1024
            ks = [('stg', i) for i in range(4) if 516 * i < hi and 516 * i + 516 > lo]
            ks += [('cv', i) for i in range(4) if 2064 + 512 * i < hi and 2576 + 512 * i > lo]
            return ph[:, lo:hi], ks

        state = {'acc': 0, 'wb': 0, 'xst': 0, 'gt': 0}

        def gate_tmp(allow_mean=True):
            q = state['gt']
            state['gt'] = 1 - q
            if q and allow_mean:
                return mean, 'mean'
            return tmpA, 'tmpA'


        def next_acc():
            i = state['acc']
            state['acc'] = (i + 1) % 3
            return acc[i], ('acc', i)

        wcache = {}

        def load_w(wd, ranges, kts, k0=0):
            i = state['wb']
            state['wb'] = (i + 1) % 3
            ncols = sum(n for _, n in ranges)
            assert kts * ncols <= WBN
            flat = wbuf[i][:, 0:kts * ncols]
            view = flat.rearrange("p (kt c) -> p kt c", kt=kts)
            ck = (wd.name, tuple(ranges), kts, k0)
            if ck in wcache:
                S.dma('sp', flat, wcache[ck][:, :], reads=[('wsc', ck)], writes=[('wb', i)], nbytes=2 * 128 * kts * ncols)
                return view, ('wb', i)
            off = 0
            for (c0, n) in ranges:
                for ka in range(0, kts, 8):
                    kb = min(kts, ka + 8)
                    src = wd[(k0 + ka) * 128:(k0 + kb) * 128, c0:c0 + n].rearrange("(kt p) c -> p kt c", p=128)
                    S.dma('pool', view[:, ka:kb, off:off + n], src, writes=[('wb', i)])
                off += n
            if USE_WCACHE:
                sc = nc.dram_tensor(f"wsc{len(wcache)}", [128, kts * ncols], BF16).ap()
                wcache[ck] = sc
                S.dma('sp', sc[:, :], flat, reads=[('wb', i)], writes=[('wsc', ck)], nbytes=2 * 128 * kts * ncols)
            return view, ('wb', i)

        S.dma('sp', pp[:], ppd[:, :], writes=['pp'])
        S.dma('sp', bc[:], bcd[:, :], writes=['bc'])
        S.op('pool', lambda e: e.memset(onesf[:], 1.0), writes=['onesf'])
        S.op('pool', lambda e: e.memset(nhalf[:], -0.5), writes=['nhalf'])
        S.op('dve', lambda e: e.tensor_scalar(out=ppH[:], in0=pp[:], scalar1=0.5, scalar2=None, op0=ALU.mult), reads=['pp'], writes=['ppH'])
        for tl, cmp_, nm, st, cm, bs in ((identf, ALU.is_equal, 'identf', -1, 1, 0), (U, ALU.is_gt, 'U', -1, 1, 0),
                                         (LT, ALU.is_gt, 'LT', 1, -1, 1)):
            S.op('pool', lambda e: e.affine_select(out=tl[:], in_=onesf[:], pattern=[[st, 128]],
                                                   compare_op=cmp_, fill=0.0, base=bs, channel_multiplier=cm),
                 reads=['onesf'], writes=[nm])
        S.op('dve', lambda e: e.tensor_copy(out=identb[:], in_=identf[:]), reads=['identf'], writes=['identb'])
        S.op('dve', lambda e: e.tensor_copy(out=Ub[:], in_=U[:]), reads=['U'], writes=['Ub'])
        S.op('dve', lambda e: e.tensor_copy(out=LTb[:], in_=LT[:]), reads=['LT'], writes=['LTb'])
        S.op('dve', lambda e: e.tensor_copy(out=onesb[:], in_=onesf[:]), reads=['onesf'], writes=['onesb'])
        S.op('dve', lambda e: e.memset(St[:], 0.0), writes=[('St', g) for g in range(8)])
        S.op('dve', lambda e: e.memset(Sb[:], 0.0), writes=[('Sb', g) for g in range(8)])
        S.op('dve', lambda e: e.memset(tails[1][:], 0.0), writes=[('tail', 1)])
        S.op('dve', lambda e: e.memset(vb[:], 0.0), writes=[('vb', j) for j in range(8)])
        S.op('dve', lambda e: e.memset(halo[:], 0.0), writes=[('halo', t) for t in range(32)])
        S.op('dve', lambda e: e.memset(suf[:], 0.0), writes=['suf'])
        S.op('dve', lambda e: e.memset(ph[:], 0.0), writes=[('stg', i) for i in range(4)] + [('cv', i) for i in range(4)])
        S.op('act', lambda e: e.activation(out=abc[:], in_=bc[:, B_ALOG:B_ALOG + 32], func=AF.Exp),
             reads=['bc'], writes=['abc'])
        S.op('dve', lambda e: e.tensor_scalar(out=abc[:], in0=abc[:], scalar1=-1.0, scalar2=None, op0=ALU.mult),
             reads=['abc'], writes=['abc'])

        def ppc(c):
            return pp[:, c:c + 1]

        def rms_rows(src_ap, src_keys, col):
            S.op('act', lambda e: e.activation(out=junk[:, 0:src_ap.shape[-1]], in_=src_ap, func=AF.Square,
                                               accum_out=ss[:, col:col + 1]),
                 reads=src_keys, writes=[('ss', col), 'junk'])

        def rstd_cols(c0, n, inv_n, eps, extra_reads=()):
            keys = [('ss', c) for c in range(c0, c0 + n)]
            rk = [('rs', c) for c in range(c0, c0 + n)]
            S.op('pool', lambda e: e.tensor_scalar(out=rs[:, c0:c0 + n], in0=ss[:, c0:c0 + n], scalar1=inv_n, scalar2=eps,
                                                   op0=ALU.mult, op1=ALU.add), reads=keys + list(extra_reads), writes=rk)
            S.op('pool', lambda e: e.tensor_tensor(out=rs[:, c0:c0 + n], in0=rs[:, c0:c0 + n], in1=nhalf[:, 0:n], op=ALU.pow),
                 reads=rk + ['nhalf'], writes=rk)

        def mm_fm(out_ap, out_key, wv, wkey, col0, rhs_fn, kts, ncol=128):
            for kt in range(kts):
                r_ap, r_keys = rhs_fn(kt)
                S.op('pe', lambda e: e.matmul(out_ap, lhsT=wv[:, kt, col0:col0 + ncol], rhs=r_ap,
                                              start=(kt == 0), stop=(kt == kts - 1)),
                     reads=[wkey] + r_keys, writes=[out_key])

        def hT_rhs(kt):
            return slot(kt), [('slot', kt)]

        def dump(name, ap, keys):
            if name in dbg_out:
                S.dma('sp', dbg_out[name], ap, reads=keys, writes=['dbg_' + name])

        for p in range(npre + nmain):
            main = p >= npre
            first_main = (p == npre)
            r0 = 2048 - npre * 512 + p * 512
            xq = 'sp' if p in (0, npre) else 'pool'
            tail_prev = tails[(p + 1) % 2]
            tail_prev_key = ('tail', (p + 1) % 2)

            def tail_rhs(kt):
                return tail_prev[:, kt, :], [tail_prev_key]

            S.tag = ('M' if main else 'P') + 'A'
            xblk = {}

            def load_x(blk):
                i = state['xst']
                state['xst'] = (i + 1) % 2
                S.dma(xq, xst[i][:], xin[r0 + blk * 128:r0 + blk * 128 + 128, :], writes=[('xst', i)])
                return xst[i][:], [('xst', i)]
            for blk in range(4):
                src_ap, src_keys = load_x(blk)
                rms_rows(src_ap, src_keys, blk)
                rstd_cols(blk, 1, 1.0 / D, 1e-6)
                for half in range(2):
                    S.op('dve', lambda e: e.tensor_scalar(out=hb[:, half * 512:half * 512 + 512],
                                                          in0=src_ap[:, half * 512:half * 512 + 512],
                                                          scalar1=rs[:, blk:blk + 1], scalar2=None, op0=ALU.mult),
                         reads=src_keys + [('rs', blk)], writes=[('hb', half)])
                for kt in range(8):
                    S.op('pe', lambda e: e.transpose(out=ptb[:, (kt % 4) * 512 + (kt // 4) * 128:(kt % 4) * 512 + (kt // 4) * 128 + 128],
                                                     in_=hb[:, kt * 128:kt * 128 + 128], identity=identb[:]),
                         reads=[('hb', kt // 4), 'identb'], writes=[('ptb', (kt % 4) // 2)])
                for q in range(4):
                    for hh in range(2):
                        kt = q + 4 * hh
                        S.op('dve', lambda e: e.tensor_scalar(out=slot(kt)[:, blk * 128:blk * 128 + 128],
                                                              in0=ptb[:, q * 512 + hh * 128:q * 512 + hh * 128 + 128],
                                                              scalar1=ppc(P_PREG + kt), scalar2=None, op0=ALU.mult),
                             reads=[('ptb', q // 2), 'pp'], writes=[('slot', kt)])
            tcur = p % 2
            for kt in range(8):
                S.op('pool', lambda e: e.tensor_copy(out=tails[tcur][:, kt, :], in_=slot(kt)[:, 480:512]),
                     reads=[('slot', kt)], writes=[('tail', tcur)])
            if p == npre + nmain - 1:
                dump('hT', slots[:, 0:4096], [('slot', k) for k in range(8)])

            if upto < 'B':
                continue
            S.tag = 'MB'
            bw = {}

            def b1(j):
                if j % 2 == 0:
                    bw['glu'] = load_w(w_in, [(j * 128, 256), (1024 + j * 128, 256)], 8)
                wv, wk = bw['glu']
                jj = j % 2
                for (rfn, n, off) in ((tail_rhs, 32, 0), (hT_rhs, 512, 32)):
                    a_v, k_v = next_acc()
                    a_g, k_g = next_acc()
                    mm_fm(a_v[:, 0:n], k_v, wv, wk, jj * 128, rfn, 8)
                    mm_fm(a_g[:, 0:n], k_g, wv, wk, 256 + jj * 128, rfn, 8)
                    gt, gk = gate_tmp()
                    S.op('act', lambda e: e.activation(out=gt[:, 0:n], in_=a_g[:, 0:n], func=AF.Tanh,
                                                       bias=ppH[:, P_GLUB + 8 + j:P_GLUB + 9 + j], scale=0.5),
                         reads=[k_g, 'ppH'], writes=[gk])
                    S.op('dve', lambda e: e.tensor_scalar(out=gt[:, 0:n], in0=gt[:, 0:n], scalar1=0.5, scalar2=0.5,
                                                          op0=ALU.mult, op1=ALU.add), reads=[gk], writes=[gk])
                    S.op('dve', lambda e: e.scalar_tensor_tensor(out=vb[:, j, off:off + n], in0=a_v[:, 0:n],
                                                                 scalar=ppc(P_GLUB + j), in1=gt[:, 0:n],
                                                                 op0=ALU.add, op1=ALU.mult),
                         reads=[k_v, gk, 'pp'], writes=[('vb', j)])
                if first_main:
                    S.op('dve', lambda e: e.tensor_scalar(out=vb[:, j, 0:32], in0=vb[:, j, 0:32],
                                                          scalar1=ppc(P_FLAG), scalar2=None, op0=ALU.mult),
                         reads=[('vb', j), 'pp'], writes=[('vb', j)])
                S.op('dve', lambda e: e.tensor_tensor(
                    out=diag[:], in0=identb[:].unsqueeze(1).broadcast_to([128, 31, 128]),
                    in1=pp[:, P_DWW + j * 31:P_DWW + j * 31 + 31].unsqueeze(2).broadcast_to([128, 31, 128]),
                    op=ALU.mult), reads=['identb', 'pp'], writes=['diag'])
                a_c, k_c = next_acc()
                for k in range(31):
                    S.op('pe', lambda e: e.matmul(a_c[:], lhsT=diag[:, k, :], rhs=vb[:, j, 2 + k:2 + k + 512],
                                                  start=(k == 0), stop=(k == 30)),
                         reads=['diag', ('vb', j)], writes=[k_c])
                vcj, vck = vc_t(j)
                S.op('act', lambda e: e.activation(out=vcj, in_=a_c[:], func=AF.Identity,
                                                   bias=ppc(P_DWB + j), scale=1.0),
                     reads=[k_c, 'pp'], writes=vck)

            def lnfin():
                for j in range(8):
                    vcj, vck = vc_t(j)
                    S.op('act', lambda e: e.activation(out=rstd[:], in_=vcj, func=AF.Square), reads=vck, writes=['rstd'])
                    S.op('pe', lambda e: e.matmul(S0[:], lhsT=onesb[:], rhs=vcj, start=(j == 0), stop=(j == 7)),
                         reads=['onesb'] + vck, writes=['S0'])
                    S.op('pe', lambda e: e.matmul(S1[:], lhsT=onesf[:], rhs=rstd[:], start=(j == 0), stop=(j == 7)),
                         reads=['onesf', 'rstd'], writes=['S1'])
                S.op('dve', lambda e: e.tensor_scalar(out=mean[:], in0=S0[:], scalar1=1.0 / 1024, scalar2=None, op0=ALU.mult),
                     reads=['S0'], writes=['mean'])
                S.op('dve', lambda e: e.tensor_tensor(out=tmpA[:], in0=mean[:], in1=mean[:], op=ALU.mult),
                     reads=['mean'], writes=['tmpA'])
                S.op('dve', lambda e: e.scalar_tensor_tensor(out=rstd[:], in0=S1[:], scalar=1.0 / 1024, in1=tmpA[:],
                                                             op0=ALU.mult, op1=ALU.subtract),
                     reads=['S1', 'tmpA'], writes=['rstd'])
                S.op('act', lambda e: e.activation(out=rstd[:], in_=rstd[:], func=AF.Sqrt, bias=1e-5, scale=1.0),
                     reads=['rstd'], writes=['rstd'])
                S.op('dve', lambda e: e.reciprocal(out=rstd[:], in_=rstd[:]), reads=['rstd'], writes=['rstd'])

            def b2(j):
                vcj, vck = vc_t(j)
                S.op('dve', lambda e: e.tensor_tensor(out=tmpA[:], in0=vcj, in1=mean[:], op=ALU.subtract),
                     reads=vck + ['mean'], writes=['tmpA'])
                S.op('dve', lambda e: e.tensor_tensor(out=tmpA[:], in0=tmpA[:], in1=rstd[:], op=ALU.mult),
                     reads=['tmpA', 'rstd'], writes=['tmpA'])
                S.op('act', lambda e: e.activation(out=vb[:, j, 0:512], in_=tmpA[:], func=AF.Silu,
                                                   bias=ppc(P_LNB + j), scale=ppc(P_LNG + j)),
                     reads=['tmpA', 'pp'], writes=[('vb', j)])

            def b3(i):
                if i % 2 == 0:
                    bw['pw'] = load_w(w_pw, [(i * 128, 256)], 8)
                    bw['gc'] = load_w(w_in, [(OFF_GATE + i * 128, 256)], 8)
                wpv, wpk = bw['pw']
                wgv, wgk = bw['gc']
                ii = i % 2
                a_p, k_p = next_acc()
                a_g, k_g = next_acc()
                mm_fm(a_p[:], k_p, wpv, wpk, ii * 128, lambda kt: (vb[:, kt, 0:512], [('vb', kt)]), 8)
                mm_fm(a_g[:], k_g, wgv, wgk, ii * 128, hT_rhs, 8)
                gt, gk = gate_tmp()
                S.op('act', lambda e: e.activation(out=gt[:], in_=a_g[:], func=AF.Tanh,
                                                   bias=ppH[:, P_GATEB + i:P_GATEB + i + 1], scale=0.5),
                     reads=[k_g, 'ppH'], writes=[gk])
                S.op('dve', lambda e: e.tensor_scalar(out=gt[:], in0=gt[:], scalar1=0.5, scalar2=0.5,
                                                      op0=ALU.mult, op1=ALU.add), reads=[gk], writes=[gk])
                S.op('dve', lambda e: e.scalar_tensor_tensor(out=slot(8 + i), in0=a_p[:], scalar=ppc(P_PWB + i),
                                                             in1=gt[:], op0=ALU.add, op1=ALU.mult),
                     reads=[k_p, gk, 'pp'], writes=[('slot', 8 + i)])

            def bstep(g):
                if not main:
                    return
                S.tag = 'MB'
                if g < 4:
                    b1(2 * g)
                    b1(2 * g + 1)
                else:
                    if g == 4:
                        lnfin()
                        for j in range(8):
                            b2(j)
                    b3(2 * (g - 4))
                    b3(2 * (g - 4) + 1)
                S.tag = 'MC'

            if upto < 'C':
                continue
            S.tag = ('M' if main else 'P') + 'C'
            wdv, wdk = load_w(w_in, [(OFF_DT, 32)], 8)
            a_d, k_d = next_acc()
            for blk in range(4):
                for kt in range(8):
                    S.op('pe', lambda e: e.matmul(a_d[:, blk * 32:blk * 32 + 32], lhsT=slot(kt)[:, blk * 128:blk * 128 + 128],
                                                  rhs=wdv[:, kt, :], start=(kt == 0), stop=(kt == 7)),
                         reads=[('slot', kt), wdk], writes=[k_d])
            S.op('dve', lambda e: e.tensor_tensor(out=dtp[:], in0=a_d[:, 0:128].rearrange("p (b h) -> p b h", b=4),
                                                  in1=bc[:, B_DTB:B_DTB + 32].unsqueeze(1).broadcast_to([128, 4, 32]),
                                                  op=ALU.add), reads=[k_d, 'bc'], writes=['dtp'])
            S.op('act', lambda e: e.activation(out=dtp[:], in_=dtp[:], func=AF.Exp), reads=['dtp'], writes=['dtp'])
            S.op('act', lambda e: e.activation(out=dtt[:], in_=dtp[:], func=AF.Ln, bias=1.0, scale=1.0),
                 reads=['dtp'], writes=['dtt'])
            S.op('dve', lambda e: e.tensor_tensor(out=adt[:], in0=dtt[:],
                                                  in1=abc[:].unsqueeze(1).broadcast_to([128, 4, 32]), op=ALU.mult),
                 reads=['dtt', 'abc'], writes=['adt'])
            S.op('dve', lambda e: e.tensor_copy(out=adthl[:, 0], in_=adt[:]), reads=['adt'], writes=['adth'])
            S.op('dve', lambda e: e.tensor_tensor(out=adthl[:, 1], in0=adt[:], in1=adthl[:, 0], op=ALU.subtract),
                 reads=['adt', 'adth'], writes=['adtl'])
            a_e, k_e = next_acc()
            for c in range(4):
                S.op('pe', lambda e: e.matmul(a_e[:, c * 32:c * 32 + 32], lhsT=U[:], rhs=adt[:, c, :], start=True, stop=True),
                     reads=['U', 'adt'], writes=[k_e])
                S.op('pe', lambda e: e.matmul(a_e[:, 128 + c * 32:128 + c * 32 + 32], lhsT=onesf[:], rhs=adt[:, c, :],
                                              start=True, stop=True), reads=['onesf', 'adt'], writes=[k_e])
            if main:
                S.op('act', lambda e: e.activation(out=dte[:], in_=a_e[:, 0:128].rearrange("p (b h) -> p b h", b=4), func=AF.Exp),
                     reads=[k_e], writes=['dte'])
                S.op('act', lambda e: e.activation(out=cdb[:], in_=a_e[:, 128:256].rearrange("p (b h) -> p b h", b=4), func=AF.Exp),
                     reads=[k_e], writes=['cdb'])
            else:
                cdp = a_e[:, 128:256].rearrange("p (b h) -> p b h", b=4)
                S.op('dve', lambda e: e.tensor_copy(out=suf[:, 2, :], in_=cdp[:, 3, :]), reads=[k_e], writes=['suf'])
                S.op('dve', lambda e: e.tensor_tensor(out=suf[:, 1, :], in0=cdp[:, 2, :], in1=suf[:, 2, :], op=ALU.add),
                     reads=[k_e, 'suf'], writes=['suf'])
                S.op('dve', lambda e: e.tensor_tensor(out=suf[:, 0, :], in0=cdp[:, 1, :], in1=suf[:, 1, :], op=ALU.add),
                     reads=[k_e, 'suf'], writes=['suf'])
                S.op('dve', lambda e: e.tensor_tensor(out=dte[:], in0=a_e[:, 0:128].rearrange("p (b h) -> p b h", b=4),
                                                      in1=suf[:], op=ALU.add), reads=[k_e, 'suf'], writes=['dte'])
                S.op('act', lambda e: e.activation(out=dte[:], in_=dte[:], func=AF.Exp), reads=['dte'], writes=['dte'])
                S.op('dve', lambda e: e.tensor_tensor(out=cdb[:, 0, :], in0=cdp[:, 0, :], in1=suf[:, 0, :], op=ALU.add),
                     reads=[k_e, 'suf'], writes=['cdb'])
                S.op('act', lambda e: e.activation(out=cdb[:, 0, :], in_=cdb[:, 0, :], func=AF.Exp), reads=['cdb'], writes=['cdb'])
            S.op('dve', lambda e: e.tensor_tensor(out=w2[:], in0=dtt[:], in1=dte[:], op=ALU.mult),
                 reads=['dtt', 'dte'], writes=['w2'])
            if first_main:
                S.op('dve', lambda e: e.tensor_scalar(out=St[:], in0=St[:], scalar1=ppc(P_FLAG), scalar2=None, op0=ALU.mult),
                     reads=[('St', g) for g in range(8)] + ['pp'], writes=[('St', g) for g in range(8)])
                S.op('act', lambda e: e.activation(out=Sb[:], in_=St[:], func=AF.Copy),
                     reads=[('St', g) for g in range(8)], writes=[('Sb', g) for g in range(8)])

            def g1_load(g):
                ranges = [(OFF_XBC + 256 * g, 256), (OFF_XBC + 2048 + 128 * g, 128)]
                if main:
                    ranges.append((OFF_XBC + 3072 + 128 * g, 128))
                w = load_w(w_in, ranges, 8)
                wz = load_w(w_in, [(OFF_Z + 256 * g, 256)], 8) if main else None
                return w, wz

            def g1_tile(g, i, w):
                wv, wk = w
                bs = g % 2
                sg, sgk = stg_t(i)
                cvi, cvk = cv_t(i)
                tix = (2 * g + i) if i < 2 else (16 + g if i == 2 else 24 + g)
                if first_main and i == 3:
                    a_h, k_h = next_acc()
                    mm_fm(a_h[:, 0:32], k_h, wv, wk, i * 128, tail_rhs, 8)
                    S.op('dve', lambda e: e.tensor_copy(out=sg[:, 0:3], in_=a_h[:, 29:32]), reads=[k_h], writes=sgk)
                else:
                    S.op('dve', lambda e: e.tensor_copy(out=sg[:, 0:3], in_=halo[:, tix, :]), reads=[('halo', tix)], writes=sgk)
                a_m, k_m = next_acc()
                mm_fm(a_m[:], k_m, wv, wk, i * 128, hT_rhs, 8)
                S.op('act', lambda e: e.activation(out=sg[:, 3:515], in_=a_m[:], func=AF.Copy), reads=[k_m], writes=sgk)
                S.op('act', lambda e: e.activation(out=halo[:, tix, :], in_=a_m[:, 509:512], func=AF.Copy), reads=[k_m], writes=[('halo', tix)])
                if True:
                    S.op('act', lambda e: e.activation(out=cvi, in_=a_m[:], func=AF.Identity, bias=ppc(P_SCB + tix),
                                                       scale=ppc(P_SCW + tix * 4 + 3)),
                         reads=[k_m, 'pp'], writes=cvk)
                    for k in range(3):
                        S.op('dve', lambda e: e.scalar_tensor_tensor(out=cvi, in0=sg[:, k:k + 512],
                                                                     scalar=ppc(P_SCW + tix * 4 + k), in1=cvi,
                                                                     op0=ALU.mult, op1=ALU.add),
                             reads=sgk + cvk + ['pp'], writes=cvk)
                else:
                    S.op('pool', lambda e: e.tensor_scalar(out=cvi, in0=sg[:, 3:515], scalar1=ppc(P_SCW + tix * 4 + 3),
                                                           scalar2=ppc(P_SCB + tix), op0=ALU.mult, op1=ALU.add),
                         reads=sgk + ['pp'], writes=cvk)
                    for k in range(3):
                        S.op('pool', lambda e: e.tensor_scalar(out=ptmp[:], in0=sg[:, k:k + 512], scalar1=ppc(P_SCW + tix * 4 + k),
                                                               scalar2=None, op0=ALU.mult),
                             reads=sgk + ['pp'], writes=['ptmp'])
                        S.op('pool', lambda e: e.tensor_tensor(out=cvi, in0=cvi, in1=ptmp[:], op=ALU.add),
                             reads=cvk + ['ptmp'], writes=cvk)
                if i < 2:
                    S.op('act', lambda e: e.activation(out=xfm[bs][:, i, :], in_=cvi, func=AF.Silu), reads=cvk, writes=[('xfm', bs, i)])
                elif i == 2:
                    S.op('act', lambda e: e.activation(out=Bfm[bs][:], in_=cvi, func=AF.Silu), reads=cvk, writes=[('Bfm', bs)])
                else:
                    S.op('act', lambda e: e.activation(out=Cfm[bs][:], in_=cvi, func=AF.Silu), reads=cvk, writes=[('Cfm', bs)])

            def g1_z(g, half, wz):
                wzv, wzk = wz
                bs = g % 2
                a_z, k_z = next_acc()
                for bb in range(2):
                    blk = half * 2 + bb
                    for kt in range(8):
                        S.op('pe', lambda e: e.matmul(a_z[:, bb * 256:bb * 256 + 256],
                                                      lhsT=slot(kt)[:, blk * 128:blk * 128 + 128], rhs=wzv[:, kt, :],
                                                      start=(kt == 0), stop=(kt == 7)),
                             reads=[('slot', kt), wzk], writes=[k_z])
                S.op('act', lambda e: e.activation(out=zs[bs][:, half * 2:half * 2 + 2, :],
                                                   in_=a_z[:].rearrange("p (b c) -> p b c", b=2), func=AF.Silu),
                     reads=[k_z], writes=[('zs', bs, half)])

            def g1_part(g, part, w, wz):
                ntile = 4 if main else 3
                if part < ntile:
                    g1_tile(g, part, w)
                if main and part in (1, 3):
                    g1_z(g, part // 2, wz)

            def front(g, c, par):
                bs = g % 2
                cs = slice(c * 128, c * 128 + 128)
                hs = slice(4 * g, 4 * g + 4)
                for i in range(2):
                    S.op('pe', lambda e: e.transpose(out=ptb[:, 1152 + i * 128:1152 + i * 128 + 128], in_=xfm[bs][:, i, cs],
                                                     identity=identb[:]),
                         reads=[('xfm', bs, i), 'identb'], writes=[('ptb', 1)])
                S.op('pe', lambda e: e.transpose(out=ptb[:, 1024:1152], in_=Bfm[bs][:, cs], identity=identb[:]),
                     reads=[('Bfm', bs), 'identb'], writes=[('ptb', 1)])
                S.op('act', lambda e: e.activation(out=Bt[par][:], in_=ptb[:, 1024:1152], func=AF.Copy),
                     reads=[('ptb', 1)], writes=[('Bt', par)])
                x3 = ptb[:, 1152:1408].rearrange("p (h q) -> p h q", h=4)
                S.op('dve', lambda e: e.tensor_tensor(out=xw[par][:].rearrange("p (h q) -> p h q", h=4), in0=x3,
                                                      in1=w2[:, c, hs].unsqueeze(2).broadcast_to([128, 4, 64]), op=ALU.mult),
                     reads=[('ptb', 1), 'w2'], writes=[('xw', par)])
                if not main:
                    return
                S.op('dve', lambda e: e.tensor_tensor(out=xdt[par][:].rearrange("p (h q) -> p h q", h=4), in0=x3,
                                                      in1=dtt[:, c, hs].unsqueeze(2).broadcast_to([128, 4, 64]), op=ALU.mult),
                     reads=[('ptb', 1), 'dtt'], writes=[('xdt', par)])
                S.op('act', lambda e: e.activation(out=xtm[par][:], in_=ptb[:, 1152:1408], func=AF.Copy),
                     reads=[('ptb', 1)], writes=[('xtm', par)])
                S.op('pe', lambda e: e.matmul(S2[:, 0:128], lhsT=Bfm[bs][:, cs], rhs=Cfm[bs][:, cs], start=True, stop=True),
                     reads=[('Bfm', bs), ('Cfm', bs)], writes=['S2'])
                S.op('dve', lambda e: e.tensor_tensor(out=CBm[par][:], in0=S2[:, 0:128], in1=LT[:], op=ALU.mult),
                     reads=['S2', 'LT'], writes=[('CBm', par)])
                S.op('dve', lambda e: e.tensor_tensor(out=Rt[par][:], in0=adthl[:, :, c, hs].unsqueeze(3).broadcast_to([128, 2, 4, 128]),
                                                      in1=LTb[:].unsqueeze(1).unsqueeze(1).broadcast_to([128, 2, 4, 128]), op=ALU.mult),
                     reads=['adth', 'adtl', 'LTb'], writes=[('Rt', par)])
                for t in range(2):
                    Rf = Rt[par][:, t].rearrange("p h l -> p (h l)")
                    S.op('pe', lambda e: e.matmul(S0[:], lhsT=Ub[:], rhs=Rf, start=(t == 0), stop=(t == 1)),
                         reads=['Ub', ('Rt', par)], writes=['S0'])
                for t in range(2):
                    Rf = Rt[par][:, t].rearrange("p h l -> p (h l)")
                    S.op('pe', lambda e: e.matmul(S1[:], lhsT=onesb[:], rhs=Rf, start=(t == 0), stop=(t == 1)),
                         reads=['onesb', ('Rt', par)], writes=['S1'])
                S.op('act', lambda e: e.activation(out=E1[par][:], in_=S0[:], func=AF.Exp), reads=['S0'], writes=[('E1', par)])
                S.op('act', lambda e: e.activation(out=E2[par][:], in_=S1[:], func=AF.Exp), reads=['S1'], writes=[('E2', par)])
                S.op('dve', lambda e: e.tensor_tensor(out=MT[par][:], in0=E1[par][:].rearrange("p (h l) -> p h l", h=4),
                                                      in1=CBm[par][:].unsqueeze(1).broadcast_to([128, 4, 128]), op=ALU.mult),
                     reads=[('E1', par), ('CBm', par)], writes=[('MT', par)])
                S.op('dve', lambda e: e.tensor_tensor(out=CE[par][:], in0=E2[par][:].rearrange("p (h l) -> p h l", h=4),
                                                      in1=Cfm[bs][:, cs].unsqueeze(1).broadcast_to([128, 4, 128]), op=ALU.mult),
                     reads=[('E2', par), ('Cfm', bs)], writes=[('CE', par)])

            def back(g, c, par):
                bs = g % 2
                cs = slice(c * 128, c * 128 + 128)
                hs = slice(4 * g, 4 * g + 4)
                if main:
                    a_y, k_y = next_acc()
                    for h in range(4):
                        S.op('pe', lambda e: e.matmul(a_y[:, h * 64:h * 64 + 64], lhsT=MT[par][:, h, :], rhs=xdt[par][:, h * 64:h * 64 + 64],
                                                      start=True, stop=False),
                             reads=[('MT', par), ('xdt', par)], writes=[k_y])
                        S.op('pe', lambda e: e.matmul(a_y[:, h * 64:h * 64 + 64], lhsT=CE[par][:, h, :],
                                                      rhs=Sb[:, (4 * g + h) * 64:(4 * g + h) * 64 + 64], start=False, stop=True),
                             reads=[('CE', par), ('Sb', g)], writes=[k_y])
                    yvp, ynp, smp = yv[par], yn[par], sm[par]
                    S.op('dve', lambda e: e.tensor_tensor(out=yvp[:].rearrange("p (h q) -> p h q", h=4),
                                                          in0=xtm[par][:].rearrange("p (h q) -> p h q", h=4),
                                                          in1=bc[:, B_DSK + 4 * g:B_DSK + 4 * g + 4].unsqueeze(2).broadcast_to([128, 4, 64]),
                                                          op=ALU.mult), reads=[('xtm', par), 'bc'], writes=[('yv', par)])
                    S.op('dve', lambda e: e.tensor_tensor(out=yvp[:], in0=a_y[:, 0:256], in1=yvp[:], op=ALU.add),
                         reads=[k_y, ('yv', par)], writes=[('yv', par)])
                    S.op('dve', lambda e: e.tensor_tensor(out=yvp[:], in0=yvp[:], in1=zs[bs][:, c, :], op=ALU.mult),
                         reads=[('yv', par), ('zs', bs, c // 2)], writes=[('yv', par)])
                    S.op('act', lambda e: e.activation(out=ynp[:], in_=yvp[:], func=AF.Square, accum_out=smp[:, 0:1]),
                         reads=[('yv', par)], writes=[('sm', par), ('yn', par)])
                    S.op('pool', lambda e: e.tensor_scalar(out=smp[:, 1:2], in0=smp[:, 0:1], scalar1=1.0 / 256, scalar2=1e-6,
                                                           op0=ALU.mult, op1=ALU.add), reads=[('sm', par)], writes=[('sm1', par)])
                    S.op('pool', lambda e: e.tensor_tensor(out=smp[:, 2:3], in0=smp[:, 1:2], in1=nhalf[:, 0:1], op=ALU.pow),
                         reads=[('sm1', par), 'nhalf'], writes=[('sm2', par)])
                    S.op('dve', lambda e: e.tensor_scalar(out=ynp[:], in0=yvp[:], scalar1=smp[:, 2:3], scalar2=None, op0=ALU.mult),
                         reads=[('yv', par), ('sm2', par)], writes=[('yn', par)])
                    for i in range(2):
                        S.op('pe', lambda e: e.transpose(out=ptb[:, 512 + i * 128:512 + i * 128 + 128], in_=ynp[:, i * 128:i * 128 + 128],
                                                         identity=identb[:]),
                             reads=[('yn', par), 'identb'], writes=[('ptb', 0)])
                    for i in range(2):
                        S.op('act', lambda e: e.activation(out=slot(16 + 2 * g + i)[:, cs], in_=ptb[:, 512 + i * 128:512 + i * 128 + 128],
                                                           func=AF.Copy, scale=ppc(P_NG + 2 * g + i)),
                             reads=[('ptb', 0), 'pp'], writes=[('slot', 16 + 2 * g + i)])
                a_s, k_s = S2[:, 128:384], 'S2'
                Sg = St[:, 256 * g:256 * g + 256]
                if main:
                    S.op('pe', lambda e: e.matmul(a_s[:, 0:256], lhsT=Bt[par][:], rhs=xw[par][:], start=True, stop=True),
                         reads=[('Bt', par), ('xw', par)], writes=[k_s])
                    cdv = cdb[:, c, hs]
                else:
                    S.op('pe', lambda e: e.matmul(a_s[:, 0:256], lhsT=Bt[par][:], rhs=xw[par][:], start=(c == 0), stop=(c == 3)),
                         reads=[('Bt', par), ('xw', par)], writes=[k_s])
                    if c < 3:
                        return
                    cdv = cdb[:, 0, hs]
                S.op('dve', lambda e: e.tensor_tensor(out=Sg.rearrange("p (h q) -> p h q", h=4),
                                                      in0=Sg.rearrange("p (h q) -> p h q", h=4),
                                                      in1=cdv.unsqueeze(2).broadcast_to([128, 4, 64]), op=ALU.mult),
                     reads=[('St', g), 'cdb'], writes=[('St', g)])
                S.op('dve', lambda e: e.tensor_tensor(out=Sg, in0=a_s[:, 0:256], in1=Sg, op=ALU.add),
                     reads=[k_s, ('St', g)], writes=[('St', g)])
                if main:
                    S.op('act', lambda e: e.activation(out=Sb[:, 256 * g:256 * g + 256], in_=Sg, func=AF.Copy),
                         reads=[('St', g)], writes=[('Sb', g)])

            wcur = g1_load(0)
            for part in range(4):
                g1_part(0, part, *wcur)
            for g in range(8):
                bstep(g)
                if g < 7:
                    wnext = g1_load(g + 1)
                front(g, 0, 0)
                for c in range(4):
                    if c < 3:
                        front(g, c + 1, (c + 1) % 2)
                    back(g, c, c % 2)
                    if g < 7:
                        g1_part(g + 1, c, *wnext)

            if not main:
                continue

            if upto < 'D':
                continue
            S.tag = 'MD'
            for i in range(8):
                if i % 2 == 0:
                    wsv, wsk = load_w(w_so, [(i * 128, 256)], 16)
                if i % 4 == 0:
                    wgv, wgk = load_w(w_in, [(OFF_GATE + 1024 + i * 128, 512)], 8)
                a_p, k_p = next_acc()
                a_g, k_g = next_acc()
                mm_fm(a_p[:], k_p, wsv, wsk, (i % 2) * 128, lambda kt: (slot(16 + kt), [('slot', 16 + kt)]), 16)
                mm_fm(a_g[:], k_g, wgv, wgk, (i % 4) * 128, hT_rhs, 8)
                gt, gk = gate_tmp()
                S.op('act', lambda e: e.activation(out=gt[:], in_=a_g[:], func=AF.Tanh,
                                                   bias=ppH[:, P_GATEB + 8 + i:P_GATEB + 9 + i], scale=0.5),
                     reads=[k_g, 'ppH'], writes=[gk])
                S.op('dve', lambda e: e.tensor_scalar(out=gt[:], in0=gt[:], scalar1=0.5, scalar2=0.5,
                                                      op0=ALU.mult, op1=ALU.add), reads=[gk], writes=[gk])
                S.op('dve', lambda e: e.tensor_tensor(out=gt[:], in0=a_p[:], in1=gt[:], op=ALU.mult),
                     reads=[k_p, gk], writes=[gk])
                S.op('dve', lambda e: e.tensor_tensor(out=slot(8 + i), in0=gt[:], in1=slot(8 + i), op=ALU.add),
                     reads=[gk, ('slot', 8 + i)], writes=[('slot', 8 + i)])

            for half in range(2):
                wov, wok = load_w(w_o, [(half * 512, 512)], 8)
                for blk in range(4):
                    a_o, k_o = next_acc()
                    for kt in range(8):
                        S.op('pe', lambda e: e.matmul(a_o[:], lhsT=slot(8 + kt)[:, blk * 128:blk * 128 + 128], rhs=wov[:, kt, :],
                                                      start=(kt == 0), stop=(kt == 7)),
                             reads=[('slot', 8 + kt), wok], writes=[k_o])
                    S.op('act', lambda e: e.activation(out=x1[:, blk, half * 512:half * 512 + 512], in_=a_o[:], func=AF.Copy),
                         reads=[k_o], writes=[('x1', blk)])
                    S.op('act', lambda e: e.activation(out=junk[:, 0:512], in_=a_o[:], func=AF.Square,
                                                       accum_out=ss[:, 4 + blk * 2 + half:5 + blk * 2 + half]),
                         reads=[k_o], writes=[('ss', 4 + blk * 2 + half), 'junk'])
            for blk in range(4):
                S.op('dve', lambda e: e.tensor_tensor(out=ss[:, 12 + blk:13 + blk], in0=ss[:, 4 + 2 * blk:5 + 2 * blk],
                                                      in1=ss[:, 5 + 2 * blk:6 + 2 * blk], op=ALU.add),
                     reads=[('ss', 4 + 2 * blk), ('ss', 5 + 2 * blk)], writes=[('ss', 12 + blk)])
            rstd_cols(12, 4, 1.0 / D, 1e-6)
            for blk in range(4):
                i = state['xst']
                state['xst'] = (i + 1) % 2
                S.dma(xq, xst[i][:], xin[r0 + blk * 128:r0 + blk * 128 + 128, :], writes=[('xst', i)])
                S.op('dve', lambda e: e.scalar_tensor_tensor(out=x1[:, blk, :], in0=x1[:, blk, :], scalar=rs[:, 12 + blk:13 + blk],
                                                             in1=bc[:, B_POSTG:B_POSTG + D], op0=ALU.mult, op1=ALU.mult),
                     reads=[('x1', blk), ('rs', 12 + blk), 'bc'], writes=[('x1', blk)])
                S.op('dve', lambda e: e.tensor_tensor(out=x1[:, blk, :], in0=x1[:, blk, :], in1=xst[i][:], op=ALU.add),
                     reads=[('x1', blk), ('xst', i)], writes=[('x1', blk)])

            if upto < 'E':
                continue
            S.tag = 'ME'
            for blk in range(4):
                src_ap, src_keys = x1[:, blk, :], [('x1', blk)]
                rms_rows(src_ap, src_keys, blk)
                rstd_cols(blk, 1, 1.0 / D, 1e-6)
                for half in range(2):
                    S.op('dve', lambda e: e.tensor_scalar(out=hb[:, half * 512:half * 512 + 512],
                                                          in0=src_ap[:, half * 512:half * 512 + 512],
                                                          scalar1=rs[:, blk:blk + 1], scalar2=None, op0=ALU.mult),
                         reads=src_keys + [('rs', blk)], writes=[('hb', half)])
                for kt in range(8):
                    S.op('pe', lambda e: e.transpose(out=ptb[:, (kt % 4) * 512 + (kt // 4) * 128:(kt % 4) * 512 + (kt // 4) * 128 + 128],
                                                     in_=hb[:, kt * 128:kt * 128 + 128], identity=identb[:]),
                         reads=[('hb', kt // 4), 'identb'], writes=[('ptb', (kt % 4) // 2)])
                for q in range(4):
                    for hh in range(2):
                        kt = q + 4 * hh
                        S.op('dve', lambda e: e.tensor_scalar(out=slot(kt)[:, blk * 128:blk * 128 + 128],
                                                              in0=ptb[:, q * 512 + hh * 128:q * 512 + hh * 128 + 128],
                                                              scalar1=ppc(P_FPREG + kt), scalar2=None, op0=ALU.mult),
                             reads=[('ptb', q // 2), 'pp'], writes=[('slot', kt)])
            for i in range(22):
                if i % 2 == 0:
                    wv, wk = load_w(w_gu, [(i * 128, 256), (FH + i * 128, 256)], 8)
                ii = i % 2
                a_g, k_g = next_acc()
                a_u, k_u = next_acc()
                mm_fm(a_g[:], k_g, wv, wk, ii * 128, hT_rhs, 8)
                mm_fm(a_u[:], k_u, wv, wk, 256 + ii * 128, hT_rhs, 8)
                gt, gk = gate_tmp()
                S.op('act', lambda e: e.activation(out=gt[:], in_=a_g[:], func=AF.Silu), reads=[k_g], writes=[gk])
                S.op('dve', lambda e: e.tensor_tensor(out=slot(8 + i), in0=a_u[:], in1=gt[:], op=ALU.mult),
                     reads=[k_u, gk], writes=[('slot', 8 + i)])
            for q in range(4):
                wv, wk = load_w(w_dn, [(q * 256, 256)], 22)
                for half in range(2):
                    a_o, k_o = next_acc()
                    for bb in range(2):
                        blk = half * 2 + bb
                        for kt in range(22):
                            S.op('pe', lambda e: e.matmul(a_o[:, bb * 256:bb * 256 + 256], lhsT=slot(8 + kt)[:, blk * 128:blk * 128 + 128],
                                                          rhs=wv[:, kt, :], start=(kt == 0), stop=(kt == 21)),
                                 reads=[('slot', 8 + kt), wk], writes=[k_o])
                    for bb in range(2):
                        blk = half * 2 + bb
                        fob, fok = fo_t(blk)
                        S.op('act', lambda e: e.activation(out=fob[:, q * 256:q * 256 + 256], in_=a_o[:, bb * 256:bb * 256 + 256], func=AF.Copy),
                             reads=[k_o], writes=fok)
            for blk in range(4):
                fob, fok = fo_t(blk)
                rms_rows(fob, fok, 8 + blk)
            rstd_cols(8, 4, 1.0 / D, 1e-6)
            for blk in range(4):
                fob, fok = fo_t(blk)
                S.op('dve', lambda e: e.scalar_tensor_tensor(out=fob, in0=fob, scalar=rs[:, 8 + blk:9 + blk],
                                                             in1=bc[:, B_FPOSTG:B_FPOSTG + D], op0=ALU.mult, op1=ALU.mult),
                     reads=fok + [('rs', 8 + blk), 'bc'], writes=fok)
                S.op('dve', lambda e: e.tensor_tensor(out=fob, in0=fob, in1=x1[:, blk, :], op=ALU.add),
                     reads=fok + [('x1', blk)], writes=fok)
                ro = (p - npre) * 512 + blk * 128
                S.dma('sp', outd[ro:ro + 128, :], fob, reads=fok, writes=[('out', p, blk)])
        S.finish('sp')
    return nc


def _prep(inputs):
    f = lambda k: np.ascontiguousarray(np.asarray(inputs[k], dtype=np.float32)[0])
    pp = np.zeros((128, NPP), np.float32)

    def pcol(v, c0):
        n = v.shape[0] // 128
        pp[:, c0:c0 + n] = v.reshape(n, 128).T
    pcol(f("gate_b"), P_GATEB)
    pcol(f("glu_b"), P_GLUB)
    dww = f("conv_dw_w")
    pp[:, P_DWW:P_DWW + 248] = dww.T.reshape(8, 128, 31).transpose(1, 0, 2).reshape(128, 248)
    pcol(f("conv_dw_b"), P_DWB)
    pcol(f("conv_ln_g"), P_LNG)
    pcol(f("conv_ln_b"), P_LNB)
    pcol(f("conv_pw_b"), P_PWB)
    scw = f("ssm_conv_w")
    pp[:, P_SCW:P_SCW + 128] = scw.T.reshape(32, 128, 4).transpose(1, 0, 2).reshape(128, 128)
    pcol(f("ssm_conv_b"), P_SCB)
    pcol(f("ssm_norm_g"), P_NG)
    pcol(f("mix_pre_g"), P_PREG)
    pcol(f("ffn_pre_g"), P_FPREG)
    bc = np.zeros((128, NBC), np.float32)
    bc[:, B_POSTG:B_POSTG + D] = f("mix_post_g")[None, :]
    bc[:, B_FPOSTG:B_FPOSTG + D] = f("ffn_post_g")[None, :]
    bc[:, B_DTB:B_DTB + 32] = f("dt_bias")[None, :]
    bc[:, B_ALOG:B_ALOG + 32] = f("a_log")[None, :]
    bc[:, B_DSK:B_DSK + 32] = f("d_skip")[None, :]
    shared = {"w_in": f("w_in"), "w_pw": f("conv_pw_w"), "w_so": f("ssm_out_w"), "w_o": f("w_out"),
              "w_gu": f("w_gate_up"), "w_dn": f("w_down"), "bc": bc}
    x = np.asarray(inputs["x"], dtype=np.float32)
    in_maps = []
    for core in range(8):
        b, hf = core // 2, core % 2
        xin = np.zeros((4096, D), np.float32)
        if hf == 1:
            xin[:] = x[b]
        else:
            xin[2048:] = x[b, :2048]
        ppc = pp.copy()
        ppc[:, P_FLAG] = float(hf)
        m = dict(shared)
        m["xin"] = xin
        m["pp"] = ppc
        in_maps.append(m)
    return in_maps


def kernel(**inputs):
    in_maps = _prep(inputs)
    nc = build_program()
    res = run_bass_kernel_spmd(nc, in_maps, core_ids=list(range(8)))
    out = np.zeros((4, 4096, D), np.float32)
    for core in range(8):
        b, hf = core // 2, core % 2
        out[b, hf * 2048:(hf + 1) * 2048] = res.results[core]["out"]
    return out
```

```python
import numpy as np
from contextlib import ExitStack
import concourse.bass as bass
import concourse.mybir as mybir
from concourse.bass_utils import run_bass_kernel_spmd

F32 = mybir.dt.float32
BF16 = mybir.dt.bfloat16
AF = mybir.ActivationFunctionType
ALU = mybir.AluOpType

D = 1024
T = 512
OFF_Z = 2048
OFF_XBC = 4096
OFF_DT = 8192
OFF_GATE = 8224
IN_COLS = 10272
FH = 2816
P_GATEB, P_GLUB, P_DWW, P_DWB, P_LNG, P_LNB, P_PWB = 0, 16, 32, 280, 288, 296, 304
P_SCW, P_SCB, P_NG, P_FLAG, P_PREG, P_FPREG, NPP = 312, 440, 472, 488, 489, 497, 512
B_POSTG, B_FPOSTG, B_DTB, B_ALOG, B_DSK, NBC = 0, 1024, 2048, 2080, 2112, 2144
NPRE = 4
NMAIN = 4
CUT = 99
LAST_SIM = None
DEBUG_SCHED = False
DMA_BW = 300.0
SCHED_WIN = 48
USE_WCACHE = True
PRIO_MODE = 2
PRIO_ALPHA = 0.05


class _Call:
    def __init__(self, name, a, kw):
        self.name, self.a, self.kw = name, a, kw


class _Rec:
    def __getattr__(self, name):
        return lambda *a, **kw: _Call(name, a, kw)


def _free(ap):
    n = 1
    for d in ap.shape[1:]:
        n *= d
    return n


class Sched:
    EXCL = set([('acc', 0), ('acc', 1), ('acc', 2), ('ptb', 0), ('ptb', 1), 'S0', 'S1', 'S2'])

    def __init__(self, nc, es, ndma=8):
        self.nc = nc
        self.eng = {'pe': nc.tensor, 'act': nc.scalar, 'dve': nc.vector,
                    'pool': nc.gpsimd, 'sp': nc.sync}
        self.sem = {k: es.enter_context(nc.semaphore('s_' + k)) for k in self.eng}
        self.dsem = {}
        for q in ('sp', 'pool'):
            self.dsem[q] = [es.enter_context(nc.semaphore(f'd_{q}{i}')) for i in range(ndma)]
        self.ops = []
        self.lastw = {}
        self.readers = {}
        self.rec = _Rec()
        self.tag = 'init'

    def _mkdeps(self, eng, reads, writes):
        deps = set()
        why = {}
        for r in reads:
            w = self.lastw.get(r)
            if w is not None:
                deps.add(w)
                why[w] = ('RAW', r)
            if r in self.EXCL:
                for i in self.readers.get(r, ()):
                    if self.ops[i]['eng'] != eng:
                        deps.add(i)
                        why[i] = ('XRD', r)
        for w_ in writes:
            w = self.lastw.get(w_)
            if w is not None:
                deps.add(w)
                why.setdefault(w, ('WAW', w_))
            for i in self.readers.get(w_, ()):
                deps.add(i)
                why.setdefault(i, ('WAR', w_))
        self._why = why
        return deps

    def _note(self, idx, reads, writes):
        for r in reads:
            self.readers.setdefault(r, []).append(idx)
        for w in writes:
            self.lastw[w] = idx
            self.readers[w] = []

    def _est(self, eng, call):
        kw = call.kw
        if eng == 'pe':
            if call.name == 'transpose':
                return 110.0
            rhs = kw.get('rhs')
            n = _free(rhs)
            t = 25.0 + n / 2.0
            if rhs.dtype == F32:
                t *= 4
            return max(t, 35.0)
        out = kw.get('out') if 'out' in kw else (call.a[0] if call.a else None)
        n = _free(out) if out is not None else 64
        if eng == 'act':
            return 230.0 + n / 1.2
        if eng == 'dve':
            return 120.0 + n / 0.96
        return 600.0 + n * 7.0

    def op(self, eng, fn, reads=(), writes=()):
        call = fn(self.rec)
        idx = len(self.ops)
        deps = self._mkdeps(eng, reads, writes)
        t = self._est(eng, call)
        tset = None
        if eng == 'act':
            f = call.kw.get('func')
            tset = {AF.Exp: 'exp', AF.Ln: 'ln', AF.Silu: 'silu', AF.Tanh: 'silu', AF.Sigmoid: 'sig', AF.Sqrt: 'sqrt'}.get(f)
        self.ops.append(dict(eng=eng, call=call, deps=deps, occ=t, lat=t + 60.0, dma=False, tag=self.tag, tset=tset, why=self._why))
        self._note(idx, reads, writes)

    def dma(self, q, out, in_, reads=(), writes=(), nbytes=None):
        idx = len(self.ops)
        deps = self._mkdeps(q, reads, writes)
        if nbytes is None:
            nbytes = 4 * 128 * _free(out) if out.shape[0] == 128 else 4 * out.shape[0] * _free(out)
        self.ops.append(dict(eng=q, call=_Call('dma_start', (), dict(out=out, in_=in_)), deps=deps,
                             occ=(900.0 if q == 'pool' else 100.0), lat=2000.0 + nbytes / DMA_BW, dma=True, nbytes=nbytes, tag=self.tag, why=self._why))
        self._note(idx, reads, writes)

    def finish(self, eng='sp', reorder=True):
        ops = self.ops
        n = len(ops)
        succ = [[] for _ in range(n)]
        ndep = [0] * n
        for i, o in enumerate(ops):
            ndep[i] = len(o['deps'])
            for d in o['deps']:
                succ[d].append(i)
        ready = {k: [] for k in self.eng}
        for i in range(n):
            if ndep[i] == 0:
                ready[ops[i]['eng']].append(i)
        blev = [0.0] * n
        for i in range(n - 1, -1, -1):
            m = 0.0
            for s_ in succ[i]:
                if blev[s_] > m:
                    m = blev[s_]
            blev[i] = m + ops[i]['lat']
        free_t = {k: 0.0 for k in self.eng}
        fin = [0.0] * n
        order = []
        dma_pipe = 0.0
        WIN = SCHED_WIN
        last_on = {}
        cur_set = None
        while len(order) < n:
            best = None
            for k, lst in ready.items():
                if not lst:
                    continue
                lst.sort()
                cand = None
                for i in (lst[:WIN] if reorder else lst[:1]):
                    dr = 0.0
                    for d in ops[i]['deps']:
                        if k == 'pe' and ops[d]['eng'] == 'pe':
                            continue
                        f = fin[d] + (0.0 if ops[d]['eng'] == k else 150.0)
                        if f > dr:
                            dr = f
                    st = max(free_t[k], dr)
                    ts_ = ops[i].get('tset')
                    if ts_ is not None and ts_ != cur_set:
                        st += 1400.0
                    if PRIO_MODE == 1:
                        key_ = (st, -blev[i], i)
                    elif PRIO_MODE == 2:
                        key_ = (st - PRIO_ALPHA * blev[i], i)
                    else:
                        key_ = (st, i)
                    if cand is None or key_ < cand[2]:
                        cand = (st, i, key_)
                if best is None or cand[:2] < best[0][:2]:
                    best = (cand, k)
            (st, i, _k), k = best
            if not reorder:
                st = 0.0
                for d in ops[i]['deps']:
                    st = max(st, fin[d])
                st = max(st, free_t[k])
            o = ops[i]
            ready[k].remove(i)
            if o.get('tset') is not None:
                cur_set = o['tset']
            if DEBUG_SCHED:
                bd, bt = None, -1.0
                for d in o['deps']:
                    f = fin[d] + (0.0 if ops[d]['eng'] == k else 150.0)
                    if f > bt:
                        bd, bt = d, f
                if free_t[k] >= bt:
                    o['bind'] = ('eng', last_on.get(k))
                else:
                    o['bind'] = ('dep', bd)
                o['st'] = st
                last_on[k] = i
            free_t[k] = st + o['occ']
            if o['dma']:
                done = max(st + 2000.0, dma_pipe) + o["nbytes"] / DMA_BW
                dma_pipe = done
                fin[i] = done
            else:
                fin[i] = st + o['lat']
            order.append(i)
            for s_ in succ[i]:
                ndep[s_] -= 1
                if ndep[s_] == 0:
                    ready[ops[s_]['eng']].append(s_)
        self.sim_end = max(fin) if fin else 0.0
        global LAST_SIM
        LAST_SIM = dict(ops=ops if DEBUG_SCHED else None, fin=fin, end=self.sim_end, busy={k: sum(o['occ'] for o in ops if o['eng'] == k) for k in self.eng})
        cnt = {k: 0 for k in self.eng}
        dcnt = {(q, i): 0 for q in self.dsem for i in range(len(self.dsem[q]))}
        drr = {q: 0 for q in self.dsem}
        known = {k: {} for k in self.eng}
        ev = [None] * n

        def semof(key):
            return self.sem[key] if isinstance(key, str) else self.dsem[key[0]][key[1]]
        for i in order:
            o = ops[i]
            k = o['eng']
            e = self.eng[k]
            need = {}
            if o['dma']:
                slot_i = drr[k]
                drr[k] = (slot_i + 1) % len(self.dsem[k])
                dkey = (k, slot_i)
                if dcnt[dkey] > 0:
                    need[dkey] = dcnt[dkey]
            for d in o['deps']:
                sk, c = ev[d]
                if sk == 'pe' and k == 'pe':
                    continue
                if need.get(sk, 0) < c:
                    need[sk] = c
            for sk, c in need.items():
                if known[k].get(sk, 0) >= c:
                    continue
                e.wait_ge(semof(sk), c * (1 if isinstance(sk, str) else 16))
                known[k][sk] = c
            inst = getattr(e, o['call'].name)(*o['call'].a, **o['call'].kw)
            if o['dma']:
                inst.then_inc(self.dsem[k][slot_i], 16)
                dcnt[dkey] += 1
                ev[i] = (dkey, dcnt[dkey])
            elif k == 'pe' and not any(ops[s_]['eng'] != 'pe' for s_ in succ[i]):
                ev[i] = ('pe', cnt['pe'] + 1)
            else:
                cnt[k] += 1
                inst.then_inc(self.sem[k], 1)
                ev[i] = (k, cnt[k])
        for k in self.eng:
            if cnt[k] and k != eng:
                self.eng[eng].wait_ge(self.sem[k], cnt[k])
        for dkey, c in dcnt.items():
            if c:
                self.eng[eng].wait_ge(self.dsem[dkey[0]][dkey[1]], c * 16)


def build_program(npre=NPRE, nmain=NMAIN, dbg=None, upto='E'):
    nc = bass.Bass("TRN2", target_bir_lowering=False)

    def din(name, shape):
        return nc.dram_tensor(name, shape, F32, kind="ExternalInput").ap()
    xin = din("xin", [4096, D])
    w_in = din("w_in", [D, IN_COLS])
    w_pw = din("w_pw", [D, D])
    w_so = din("w_so", [2048, D])
    w_o = din("w_o", [D, D])
    w_gu = din("w_gu", [D, 2 * FH])
    w_dn = din("w_dn", [FH, D])
    ppd = din("pp", [128, NPP])
    bcd = din("bc", [128, NBC])
    outd = nc.dram_tensor("out", [2048, D], F32, kind="ExternalOutput").ap()
    dbg_out = {}
    if dbg:
        for name, shape in dbg.items():
            dbg_out[name] = nc.dram_tensor("dbg_" + name, shape, F32, kind="ExternalOutput").ap()

    es = ExitStack()
    with es:
        S = Sched(nc, es)

        def A(name, shape, dt=F32):
            return nc.alloc_sbuf_tensor("sb_" + name, shape, dt)
        identf = A("identf", [128, 128])
        identb = A("identb", [128, 128], BF16)
        U = A("U", [128, 128])
        LT = A("LT", [128, 128])
        onesf = A("onesf", [128, 128])
        pp = A("pp", [128, NPP])
        bc = A("bc", [128, NBC])
        abc = A("abc", [128, 32])
        slots = A("slots", [128, 32 * 512], BF16)
        WBN = 5632
        wbuf = [A(f"wb{i}", [128, WBN], BF16) for i in range(3)]
        xst = [A(f"xst{i}", [128, D]) for i in range(2)]
        x1 = A("x1", [128, 4, D])
        hb = A("hb", [128, D], BF16)
        junk = A("junk", [128, D], BF16)
        ss = A("ss", [128, 16])
        rs = A("rs", [128, 16])
        tails = [A(f"tail{i}", [128, 8, 32], BF16) for i in range(2)]
        vb = A("vb", [128, 8, 544], BF16)
        diag = A("diag", [128, 31, 128], BF16)
        ph = A("ph", [128, 4224])
        mean = A("mean", [128, 512])
        rstd = A("rstd", [128, 512])
        tmpA = A("tmpA", [128, 512])
        vcb = A("vcb", [128, 8, 512], BF16)
        halo = A("halo", [128, 32, 3])
        suf = A("suf", [128, 4, 32])
        xfm = [A(f"xfm{i}", [128, 2, 512], BF16) for i in range(2)]
        Bfm = [A(f"Bfm{i}", [128, 512], BF16) for i in range(2)]
        Cfm = [A(f"Cfm{i}", [128, 512], BF16) for i in range(2)]
        zs = [A(f"zs{i}", [128, 4, 256], BF16) for i in range(2)]
        ptmp = A("ptmp", [128, 512])
        St = A("St", [128, 2048])
        Sb = A("Sb", [128, 2048], BF16)
        dtp = A("dtp", [128, 4, 32])
        dtt = A("dtt", [128, 4, 32])
        adt = A("adt", [128, 4, 32])
        dte = A("dte", [128, 4, 32])
        cdb = A("cdb", [128, 4, 32])
        w2 = A("w2", [128, 4, 32])
        Rt = [A(f"Rt{i}", [128, 2, 4, 128], BF16) for i in range(2)]
        adthl = A("adthl", [128, 2, 4, 32], BF16)
        Ub = A("Ub", [128, 128], BF16)
        LTb = A("LTb", [128, 128], BF16)
        onesb = A("onesb", [128, 128], BF16)
        E1 = [A(f"E1{i}", [128, 512], BF16) for i in range(2)]
        E2 = [A(f"E2{i}", [128, 512], BF16) for i in range(2)]
        MT = [A(f"MT{i}", [128, 4, 128], BF16) for i in range(2)]
        CE = [A(f"CE{i}", [128, 4, 128], BF16) for i in range(2)]
        CBm = [A(f"CBm{i}", [128, 128], BF16) for i in range(2)]
        Bt = [A(f"Bt{i}", [128, 128], BF16) for i in range(2)]
        xdt = [A(f"xdt{i}", [128, 256], BF16) for i in range(2)]
        xw = [A(f"xw{i}", [128, 256], BF16) for i in range(2)]
        xtm = [A(f"xtm{i}", [128, 256]) for i in range(2)]
        yv = [A(f"yv{i}", [128, 256]) for i in range(2)]
        yn = [A(f"yn{i}", [128, 256], BF16) for i in range(2)]
        sm = [A(f"sm{i}", [128, 8]) for i in range(2)]
        nhalf = A("nhalf", [128, 4])
        ppH = A("ppH", [128, NPP])

        acc = [nc.alloc_psum_tensor(f"acc{i}", [128, 512], F32) for i in range(3)]
        ptb = nc.alloc_psum_tensor("ptb", [128, 2048], BF16)
        S0 = nc.alloc_psum_tensor("S0", [128, 512], F32)
        S1 = nc.alloc_psum_tensor("S1", [128, 512], F32)
        S2 = nc.alloc_psum_tensor("S2", [128, 512], F32)

        def slot(i):
            return slots[:, i * 512:(i + 1) * 512]

        def phk(lo, hi):
            return [('ph', b) for b in range(lo // 512, (hi - 1) // 512 + 1)]

        def vc_t(j):
            return vcb[:, j, :], [('vc', j)]

        def stg_t(i):
            return ph[:, 516 * i:516 * i + 516], [('stg', i)]

        def cv_t(i):
            return ph[:, 2064 + 512 * i:2064 + 512 * i + 512], [('cv', i)]

        def fo_t(blk):
            lo, hi = 1024 * blk, 1024 * blk + 1024
            ks = [('stg', i) for i in range(4) if 516 * i < hi and 516 * i + 516 > lo]
            ks += [('cv', i) for i in range(4) if 2064 + 512 * i < hi and 2576 + 512 * i > lo]
            return ph[:, lo:hi], ks

        state = {'acc': 0, 'wb': 0, 'xst': 0, 'gt': 0}

        def gate_tmp(allow_mean=True):
            q = state['gt']
            state['gt'] = 1 - q
            if q and allow_mean:
                return mean, 'mean'
            return tmpA, 'tmpA'


        def next_acc():
            i = state['acc']
            state['acc'] = (i + 1) % 3
            return acc[i], ('acc', i)

        wcache = {}

        def load_w(wd, ranges, kts, k0=0):
            i = state['wb']
            state['wb'] = (i + 1) % 3
            ncols = sum(n for _, n in ranges)
            assert kts * ncols <= WBN
            flat = wbuf[i][:, 0:kts * ncols]
            view = flat.rearrange("p (kt c) -> p kt c", kt=kts)
            ck = (wd.name, tuple(ranges), kts, k0)
            if ck in wcache:
                S.dma('sp', flat, wcache[ck][:, :], reads=[('wsc', ck)], writes=[('wb', i)], nbytes=2 * 128 * kts * ncols)
                return view, ('wb', i)
            off = 0
            for (c0, n) in ranges:
                for ka in range(0, kts, 8):
                    kb = min(kts, ka + 8)
                    src = wd[(k0 + ka) * 128:(k0 + kb) * 128, c0:c0 + n].rearrange("(kt p) c -> p kt c", p=128)
                    S.dma('pool', view[:, ka:kb, off:off + n], src, writes=[('wb', i)])
                off += n
            if USE_WCACHE:
                sc = nc.dram_tensor(f"wsc{len(wcache)}", [128, kts * ncols], BF16).ap()
                wcache[ck] = sc
                S.dma('sp', sc[:, :], flat, reads=[('wb', i)], writes=[('wsc', ck)], nbytes=2 * 128 * kts * ncols)
            return view, ('wb', i)

        S.dma('sp', pp[:], ppd[:, :], writes=['pp'])
        S.dma('sp', bc[:], bcd[:, :], writes=['bc'])
        S.op('pool', lambda e: e.memset(onesf[:], 1.0), writes=['onesf'])
        S.op('pool', lambda e: e.memset(nhalf[:], -0.5), writes=['nhalf'])
        S.op('dve', lambda e: e.tensor_scalar(out=ppH[:], in0=pp[:], scalar1=0.5, scalar2=None, op0=ALU.mult), reads=['pp'], writes=['ppH'])
        for tl, cmp_, nm, st, cm, bs in ((identf, ALU.is_equal, 'identf', -1, 1, 0), (U, ALU.is_gt, 'U', -1, 1, 0),
                                         (LT, ALU.is_gt, 'LT', 1, -1, 1)):
            S.op('pool', lambda e: e.affine_select(out=tl[:], in_=onesf[:], pattern=[[st, 128]],
                                                   compare_op=cmp_, fill=0.0, base=bs, channel_multiplier=cm),
                 reads=['onesf'], writes=[nm])
        S.op('dve', lambda e: e.tensor_copy(out=identb[:], in_=identf[:]), reads=['identf'], writes=['identb'])
        S.op('dve', lambda e: e.tensor_copy(out=Ub[:], in_=U[:]), reads=['U'], writes=['Ub'])
        S.op('dve', lambda e: e.tensor_copy(out=LTb[:], in_=LT[:]), reads=['LT'], writes=['LTb'])
        S.op('dve', lambda e: e.tensor_copy(out=onesb[:], in_=onesf[:]), reads=['onesf'], writes=['onesb'])
        S.op('dve', lambda e: e.memset(St[:], 0.0), writes=[('St', g) for g in range(8)])
        S.op('dve', lambda e: e.memset(Sb[:], 0.0), writes=[('Sb', g) for g in range(8)])
        S.op('dve', lambda e: e.memset(tails[1][:], 0.0), writes=[('tail', 1)])
        S.op('dve', lambda e: e.memset(vb[:], 0.0), writes=[('vb', j) for j in range(8)])
        S.op('dve', lambda e: e.memset(halo[:], 0.0), writes=[('halo', t) for t in range(32)])
        S.op('dve', lambda e: e.memset(suf[:], 0.0), writes=['suf'])
        S.op('dve', lambda e: e.memset(ph[:], 0.0), writes=[('stg', i) for i in range(4)] + [('cv', i) for i in range(4)])
        S.op('act', lambda e: e.activation(out=abc[:], in_=bc[:, B_ALOG:B_ALOG + 32], func=AF.Exp),
             reads=['bc'], writes=['abc'])
        S.op('dve', lambda e: e.tensor_scalar(out=abc[:], in0=abc[:], scalar1=-1.0, scalar2=None, op0=ALU.mult),
             reads=['abc'], writes=['abc'])

        def ppc(c):
            return pp[:, c:c + 1]

        def rms_rows(src_ap, src_keys, col):
            S.op('act', lambda e: e.activation(out=junk[:, 0:src_ap.shape[-1]], in_=src_ap, func=AF.Square,
                                               accum_out=ss[:, col:col + 1]),
                 reads=src_keys, writes=[('ss', col), 'junk'])

        def rstd_cols(c0, n, inv_n, eps, extra_reads=()):
            keys = [('ss', c) for c in range(c0, c0 + n)]
            rk = [('rs', c) for c in range(c0, c0 + n)]
            S.op('pool', lambda e: e.tensor_scalar(out=rs[:, c0:c0 + n], in0=ss[:, c0:c0 + n], scalar1=inv_n, scalar2=eps,
                                                   op0=ALU.mult, op1=ALU.add), reads=keys + list(extra_reads), writes=rk)
            S.op('pool', lambda e: e.tensor_tensor(out=rs[:, c0:c0 + n], in0=rs[:, c0:c0 + n], in1=nhalf[:, 0:n], op=ALU.pow),
                 reads=rk + ['nhalf'], writes=rk)

        def mm_fm(out_ap, out_key, wv, wkey, col0, rhs_fn, kts, ncol=128):
            for kt in range(kts):
                r_ap, r_keys = rhs_fn(kt)
                S.op('pe', lambda e: e.matmul(out_ap, lhsT=wv[:, kt, col0:col0 + ncol], rhs=r_ap,
                                              start=(kt == 0), stop=(kt == kts - 1)),
                     reads=[wkey] + r_keys, writes=[out_key])

        def hT_rhs(kt):
            return slot(kt), [('slot', kt)]

        def dump(name, ap, keys):
            if name in dbg_out:
                S.dma('sp', dbg_out[name], ap, reads=keys, writes=['dbg_' + name])

        for p in range(npre + nmain):
            main = p >= npre
            first_main = (p == npre)
            r0 = 2048 - npre * 512 + p * 512
            tail_prev = tails[(p + 1) % 2]
            tail_prev_key = ('tail', (p + 1) % 2)

            def tail_rhs(kt):
                return tail_prev[:, kt, :], [tail_prev_key]

            S.tag = ('M' if main else 'P') + 'A'
            xblk = {}

            def load_x(blk):
                i = state['xst']
                state['xst'] = (i + 1) % 2
                S.dma('sp', xst[i][:], xin[r0 + blk * 128:r0 + blk * 128 + 128, :], writes=[('xst', i)])
                return xst[i][:], [('xst', i)]
            for blk in range(4):
                src_ap, src_keys = load_x(blk)
                rms_rows(src_ap, src_keys, blk)
                rstd_cols(blk, 1, 1.0 / D, 1e-6)
                for half in range(2):
                    S.op('dve', lambda e: e.tensor_scalar(out=hb[:, half * 512:half * 512 + 512],
                                                          in0=src_ap[:, half * 512:half * 512 + 512],
                                                          scalar1=rs[:, blk:blk + 1], scalar2=None, op0=ALU.mult),
                         reads=src_keys + [('rs', blk)], writes=[('hb', half)])
                for kt in range(8):
                    S.op('pe', lambda e: e.transpose(out=ptb[:, (kt % 4) * 512 + (kt // 4) * 128:(kt % 4) * 512 + (kt // 4) * 128 + 128],
                                                     in_=hb[:, kt * 128:kt * 128 + 128], identity=identb[:]),
                         reads=[('hb', kt // 4), 'identb'], writes=[('ptb', (kt % 4) // 2)])
                for q in range(4):
                    for hh in range(2):
                        kt = q + 4 * hh
                        S.op('dve', lambda e: e.tensor_scalar(out=slot(kt)[:, blk * 128:blk * 128 + 128],
                                                              in0=ptb[:, q * 512 + hh * 128:q * 512 + hh * 128 + 128],
                                                              scalar1=ppc(P_PREG + kt), scalar2=None, op0=ALU.mult),
                             reads=[('ptb', q // 2), 'pp'], writes=[('slot', kt)])
            tcur = p % 2
            for kt in range(8):
                S.op('pool', lambda e: e.tensor_copy(out=tails[tcur][:, kt, :], in_=slot(kt)[:, 480:512]),
                     reads=[('slot', kt)], writes=[('tail', tcur)])
            if p == npre + nmain - 1:
                dump('hT', slots[:, 0:4096], [('slot', k) for k in range(8)])

            if upto < 'B':
                continue
            S.tag = 'MB'
            bw = {}

            def b1(j):
                if j % 2 == 0:
                    bw['glu'] = load_w(w_in, [(j * 128, 256), (1024 + j * 128, 256)], 8)
                wv, wk = bw['glu']
                jj = j % 2
                for (rfn, n, off) in ((tail_rhs, 32, 0), (hT_rhs, 512, 32)):
                    a_v, k_v = next_acc()
                    a_g, k_g = next_acc()
                    mm_fm(a_v[:, 0:n], k_v, wv, wk, jj * 128, rfn, 8)
                    mm_fm(a_g[:, 0:n], k_g, wv, wk, 256 + jj * 128, rfn, 8)
                    gt, gk = gate_tmp()
                    S.op('act', lambda e: e.activation(out=gt[:, 0:n], in_=a_g[:, 0:n], func=AF.Tanh,
                                                       bias=ppH[:, P_GLUB + 8 + j:P_GLUB + 9 + j], scale=0.5),
                         reads=[k_g, 'ppH'], writes=[gk])
                    S.op('dve', lambda e: e.tensor_scalar(out=gt[:, 0:n], in0=gt[:, 0:n], scalar1=0.5, scalar2=0.5,
                                                          op0=ALU.mult, op1=ALU.add), reads=[gk], writes=[gk])
                    S.op('dve', lambda e: e.scalar_tensor_tensor(out=vb[:, j, off:off + n], in0=a_v[:, 0:n],
                                                                 scalar=ppc(P_GLUB + j), in1=gt[:, 0:n],
                                                                 op0=ALU.add, op1=ALU.mult),
                         reads=[k_v, gk, 'pp'], writes=[('vb', j)])
                if first_main:
                    S.op('dve', lambda e: e.tensor_scalar(out=vb[:, j, 0:32], in0=vb[:, j, 0:32],
                                                          scalar1=ppc(P_FLAG), scalar2=None, op0=ALU.mult),
                         reads=[('vb', j), 'pp'], writes=[('vb', j)])
                S.op('dve', lambda e: e.tensor_tensor(
                    out=diag[:], in0=identb[:].unsqueeze(1).broadcast_to([128, 31, 128]),
                    in1=pp[:, P_DWW + j * 31:P_DWW + j * 31 + 31].unsqueeze(2).broadcast_to([128, 31, 128]),
                    op=ALU.mult), reads=['identb', 'pp'], writes=['diag'])
                a_c, k_c = next_acc()
                for k in range(31):
                    S.op('pe', lambda e: e.matmul(a_c[:], lhsT=diag[:, k, :], rhs=vb[:, j, 2 + k:2 + k + 512],
                                                  start=(k == 0), stop=(k == 30)),
                         reads=['diag', ('vb', j)], writes=[k_c])
                vcj, vck = vc_t(j)
                S.op('act', lambda e: e.activation(out=vcj, in_=a_c[:], func=AF.Identity,
                                                   bias=ppc(P_DWB + j), scale=1.0),
                     reads=[k_c, 'pp'], writes=vck)

            def lnfin():
                for j in range(8):
                    vcj, vck = vc_t(j)
                    S.op('act', lambda e: e.activation(out=rstd[:], in_=vcj, func=AF.Square), reads=vck, writes=['rstd'])
                    S.op('pe', lambda e: e.matmul(S0[:], lhsT=onesb[:], rhs=vcj, start=(j == 0), stop=(j == 7)),
                         reads=['onesb'] + vck, writes=['S0'])
                    S.op('pe', lambda e: e.matmul(S1[:], lhsT=onesf[:], rhs=rstd[:], start=(j == 0), stop=(j == 7)),
                         reads=['onesf', 'rstd'], writes=['S1'])
                S.op('dve', lambda e: e.tensor_scalar(out=mean[:], in0=S0[:], scalar1=1.0 / 1024, scalar2=None, op0=ALU.mult),
                     reads=['S0'], writes=['mean'])
                S.op('dve', lambda e: e.tensor_tensor(out=tmpA[:], in0=mean[:], in1=mean[:], op=ALU.mult),
                     reads=['mean'], writes=['tmpA'])
                S.op('dve', lambda e: e.scalar_tensor_tensor(out=rstd[:], in0=S1[:], scalar=1.0 / 1024, in1=tmpA[:],
                                                             op0=ALU.mult, op1=ALU.subtract),
                     reads=['S1', 'tmpA'], writes=['rstd'])
                S.op('act', lambda e: e.activation(out=rstd[:], in_=rstd[:], func=AF.Sqrt, bias=1e-5, scale=1.0),
                     reads=['rstd'], writes=['rstd'])
                S.op('dve', lambda e: e.reciprocal(out=rstd[:], in_=rstd[:]), reads=['rstd'], writes=['rstd'])

            def b2(j):
                vcj, vck = vc_t(j)
                S.op('dve', lambda e: e.tensor_tensor(out=tmpA[:], in0=vcj, in1=mean[:], op=ALU.subtract),
                     reads=vck + ['mean'], writes=['tmpA'])
                S.op('dve', lambda e: e.tensor_tensor(out=tmpA[:], in0=tmpA[:], in1=rstd[:], op=ALU.mult),
                     reads=['tmpA', 'rstd'], writes=['tmpA'])
                S.op('act', lambda e: e.activation(out=vb[:, j, 0:512], in_=tmpA[:], func=AF.Silu,
                                                   bias=ppc(P_LNB + j), scale=ppc(P_LNG + j)),
                     reads=['tmpA', 'pp'], writes=[('vb', j)])

            def b3(i):
                if i % 2 == 0:
                    bw['pw'] = load_w(w_pw, [(i * 128, 256)], 8)
                    bw['gc'] = load_w(w_in, [(OFF_GATE + i * 128, 256)], 8)
                wpv, wpk = bw['pw']
                wgv, wgk = bw['gc']
                ii = i % 2
                a_p, k_p = next_acc()
                a_g, k_g = next_acc()
                mm_fm(a_p[:], k_p, wpv, wpk, ii * 128, lambda kt: (vb[:, kt, 0:512], [('vb', kt)]), 8)
                mm_fm(a_g[:], k_g, wgv, wgk, ii * 128, hT_rhs, 8)
                gt, gk = gate_tmp()
                S.op('act', lambda e: e.activation(out=gt[:], in_=a_g[:], func=AF.Tanh,
                                                   bias=ppH[:, P_GATEB + i:P_GATEB + i + 1], scale=0.5),
                     reads=[k_g, 'ppH'], writes=[gk])
                S.op('dve', lambda e: e.tensor_scalar(out=gt[:], in0=gt[:], scalar1=0.5, scalar2=0.5,
                                                      op0=ALU.mult, op1=ALU.add), reads=[gk], writes=[gk])
                S.op('dve', lambda e: e.scalar_tensor_tensor(out=slot(8 + i), in0=a_p[:], scalar=ppc(P_PWB + i),
                                                             in1=gt[:], op0=ALU.add, op1=ALU.mult),
                     reads=[k_p, gk, 'pp'], writes=[('slot', 8 + i)])

            def bstep(g):
                if not main:
                    return
                S.tag = 'MB'
                if g < 4:
                    b1(2 * g)
                    b1(2 * g + 1)
                else:
                    if g == 4:
                        lnfin()
                        for j in range(8):
                            b2(j)
                    b3(2 * (g - 4))
                    b3(2 * (g - 4) + 1)
                S.tag = 'MC'

            if upto < 'C':
                continue
            S.tag = ('M' if main else 'P') + 'C'
            wdv, wdk = load_w(w_in, [(OFF_DT, 32)], 8)
            a_d, k_d = next_acc()
            for blk in range(4):
                for kt in range(8):
                    S.op('pe', lambda e: e.matmul(a_d[:, blk * 32:blk * 32 + 32], lhsT=slot(kt)[:, blk * 128:blk * 128 + 128],
                                                  rhs=wdv[:, kt, :], start=(kt == 0), stop=(kt == 7)),
                         reads=[('slot', kt), wdk], writes=[k_d])
            S.op('dve', lambda e: e.tensor_tensor(out=dtp[:], in0=a_d[:, 0:128].rearrange("p (b h) -> p b h", b=4),
                                                  in1=bc[:, B_DTB:B_DTB + 32].unsqueeze(1).broadcast_to([128, 4, 32]),
                                                  op=ALU.add), reads=[k_d, 'bc'], writes=['dtp'])
            S.op('act', lambda e: e.activation(out=dtp[:], in_=dtp[:], func=AF.Exp), reads=['dtp'], writes=['dtp'])
            S.op('act', lambda e: e.activation(out=dtt[:], in_=dtp[:], func=AF.Ln, bias=1.0, scale=1.0),
                 reads=['dtp'], writes=['dtt'])
            S.op('dve', lambda e: e.tensor_tensor(out=adt[:], in0=dtt[:],
                                                  in1=abc[:].unsqueeze(1).broadcast_to([128, 4, 32]), op=ALU.mult),
                 reads=['dtt', 'abc'], writes=['adt'])
            S.op('dve', lambda e: e.tensor_copy(out=adthl[:, 0], in_=adt[:]), reads=['adt'], writes=['adth'])
            S.op('dve', lambda e: e.tensor_tensor(out=adthl[:, 1], in0=adt[:], in1=adthl[:, 0], op=ALU.subtract),
                 reads=['adt', 'adth'], writes=['adtl'])
            a_e, k_e = next_acc()
            for c in range(4):
                S.op('pe', lambda e: e.matmul(a_e[:, c * 32:c * 32 + 32], lhsT=U[:], rhs=adt[:, c, :], start=True, stop=True),
                     reads=['U', 'adt'], writes=[k_e])
                S.op('pe', lambda e: e.matmul(a_e[:, 128 + c * 32:128 + c * 32 + 32], lhsT=onesf[:], rhs=adt[:, c, :],
                                              start=True, stop=True), reads=['onesf', 'adt'], writes=[k_e])
            if main:
                S.op('act', lambda e: e.activation(out=dte[:], in_=a_e[:, 0:128].rearrange("p (b h) -> p b h", b=4), func=AF.Exp),
                     reads=[k_e], writes=['dte'])
                S.op('act', lambda e: e.activation(out=cdb[:], in_=a_e[:, 128:256].rearrange("p (b h) -> p b h", b=4), func=AF.Exp),
                     reads=[k_e], writes=['cdb'])
            else:
                cdp = a_e[:, 128:256].rearrange("p (b h) -> p b h", b=4)
                S.op('dve', lambda e: e.tensor_copy(out=suf[:, 2, :], in_=cdp[:, 3, :]), reads=[k_e], writes=['suf'])
                S.op('dve', lambda e: e.tensor_tensor(out=suf[:, 1, :], in0=cdp[:, 2, :], in1=suf[:, 2, :], op=ALU.add),
                     reads=[k_e, 'suf'], writes=['suf'])
                S.op('dve', lambda e: e.tensor_tensor(out=suf[:, 0, :], in0=cdp[:, 1, :], in1=suf[:, 1, :], op=ALU.add),
                     reads=[k_e, 'suf'], writes=['suf'])
                S.op('dve', lambda e: e.tensor_tensor(out=dte[:], in0=a_e[:, 0:128].rearrange("p (b h) -> p b h", b=4),
                                                      in1=suf[:], op=ALU.add), reads=[k_e, 'suf'], writes=['dte'])
                S.op('act', lambda e: e.activation(out=dte[:], in_=dte[:], func=AF.Exp), reads=['dte'], writes=['dte'])
                S.op('dve', lambda e: e.tensor_tensor(out=cdb[:, 0, :], in0=cdp[:, 0, :], in1=suf[:, 0, :], op=ALU.add),
                     reads=[k_e, 'suf'], writes=['cdb'])
                S.op('act', lambda e: e.activation(out=cdb[:, 0, :], in_=cdb[:, 0, :], func=AF.Exp), reads=['cdb'], writes=['cdb'])
            S.op('dve', lambda e: e.tensor_tensor(out=w2[:], in0=dtt[:], in1=dte[:], op=ALU.mult),
                 reads=['dtt', 'dte'], writes=['w2'])
            if first_main:
                S.op('dve', lambda e: e.tensor_scalar(out=St[:], in0=St[:], scalar1=ppc(P_FLAG), scalar2=None, op0=ALU.mult),
                     reads=[('St', g) for g in range(8)] + ['pp'], writes=[('St', g) for g in range(8)])
                S.op('act', lambda e: e.activation(out=Sb[:], in_=St[:], func=AF.Copy),
                     reads=[('St', g) for g in range(8)], writes=[('Sb', g) for g in range(8)])

            def g1_load(g):
                ranges = [(OFF_XBC + 256 * g, 256), (OFF_XBC + 2048 + 128 * g, 128)]
                if main:
                    ranges.append((OFF_XBC + 3072 + 128 * g, 128))
                w = load_w(w_in, ranges, 8)
                wz = load_w(w_in, [(OFF_Z + 256 * g, 256)], 8) if main else None
                return w, wz

            def g1_tile(g, i, w):
                wv, wk = w
                bs = g % 2
                sg, sgk = stg_t(i)
                cvi, cvk = cv_t(i)
                tix = (2 * g + i) if i < 2 else (16 + g if i == 2 else 24 + g)
                if first_main and i == 3:
                    a_h, k_h = next_acc()
                    mm_fm(a_h[:, 0:32], k_h, wv, wk, i * 128, tail_rhs, 8)
                    S.op('dve', lambda e: e.tensor_copy(out=sg[:, 0:3], in_=a_h[:, 29:32]), reads=[k_h], writes=sgk)
                else:
                    S.op('dve', lambda e: e.tensor_copy(out=sg[:, 0:3], in_=halo[:, tix, :]), reads=[('halo', tix)], writes=sgk)
                a_m, k_m = next_acc()
                mm_fm(a_m[:], k_m, wv, wk, i * 128, hT_rhs, 8)
                S.op('act', lambda e: e.activation(out=sg[:, 3:515], in_=a_m[:], func=AF.Copy), reads=[k_m], writes=sgk)
                S.op('act', lambda e: e.activation(out=halo[:, tix, :], in_=a_m[:, 509:512], func=AF.Copy), reads=[k_m], writes=[('halo', tix)])
                if True:
                    S.op('act', lambda e: e.activation(out=cvi, in_=a_m[:], func=AF.Identity, bias=ppc(P_SCB + tix),
                                                       scale=ppc(P_SCW + tix * 4 + 3)),
                         reads=[k_m, 'pp'], writes=cvk)
                    for k in range(3):
                        S.op('dve', lambda e: e.scalar_tensor_tensor(out=cvi, in0=sg[:, k:k + 512],
                                                                     scalar=ppc(P_SCW + tix * 4 + k), in1=cvi,
                                                                     op0=ALU.mult, op1=ALU.add),
                             reads=sgk + cvk + ['pp'], writes=cvk)
                else:
                    S.op('pool', lambda e: e.tensor_scalar(out=cvi, in0=sg[:, 3:515], scalar1=ppc(P_SCW + tix * 4 + 3),
                                                           scalar2=ppc(P_SCB + tix), op0=ALU.mult, op1=ALU.add),
                         reads=sgk + ['pp'], writes=cvk)
                    for k in range(3):
                        S.op('pool', lambda e: e.tensor_scalar(out=ptmp[:], in0=sg[:, k:k + 512], scalar1=ppc(P_SCW + tix * 4 + k),
                                                               scalar2=None, op0=ALU.mult),
                             reads=sgk + ['pp'], writes=['ptmp'])
                        S.op('pool', lambda e: e.tensor_tensor(out=cvi, in0=cvi, in1=ptmp[:], op=ALU.add),
                             reads=cvk + ['ptmp'], writes=cvk)
                if i < 2:
                    S.op('act', lambda e: e.activation(out=xfm[bs][:, i, :], in_=cvi, func=AF.Silu), reads=cvk, writes=[('xfm', bs, i)])
                elif i == 2:
                    S.op('act', lambda e: e.activation(out=Bfm[bs][:], in_=cvi, func=AF.Silu), reads=cvk, writes=[('Bfm', bs)])
                else:
                    S.op('act', lambda e: e.activation(out=Cfm[bs][:], in_=cvi, func=AF.Silu), reads=cvk, writes=[('Cfm', bs)])

            def g1_z(g, half, wz):
                wzv, wzk = wz
                bs = g % 2
                a_z, k_z = next_acc()
                for bb in range(2):
                    blk = half * 2 + bb
                    for kt in range(8):
                        S.op('pe', lambda e: e.matmul(a_z[:, bb * 256:bb * 256 + 256],
                                                      lhsT=slot(kt)[:, blk * 128:blk * 128 + 128], rhs=wzv[:, kt, :],
                                                      start=(kt == 0), stop=(kt == 7)),
                             reads=[('slot', kt), wzk], writes=[k_z])
                S.op('act', lambda e: e.activation(out=zs[bs][:, half * 2:half * 2 + 2, :],
                                                   in_=a_z[:].rearrange("p (b c) -> p b c", b=2), func=AF.Silu),
                     reads=[k_z], writes=[('zs', bs, half)])

            def g1_part(g, part, w, wz):
                ntile = 4 if main else 3
                if part < ntile:
                    g1_tile(g, part, w)
                if main and part in (1, 3):
                    g1_z(g, part // 2, wz)

            def front(g, c, par):
                bs = g % 2
                cs = slice(c * 128, c * 128 + 128)
                hs = slice(4 * g, 4 * g + 4)
                for i in range(2):
                    S.op('pe', lambda e: e.transpose(out=ptb[:, 1152 + i * 128:1152 + i * 128 + 128], in_=xfm[bs][:, i, cs],
                                                     identity=identb[:]),
                         reads=[('xfm', bs, i), 'identb'], writes=[('ptb', 1)])
                S.op('pe', lambda e: e.transpose(out=ptb[:, 1024:1152], in_=Bfm[bs][:, cs], identity=identb[:]),
                     reads=[('Bfm', bs), 'identb'], writes=[('ptb', 1)])
                S.op('act', lambda e: e.activation(out=Bt[par][:], in_=ptb[:, 1024:1152], func=AF.Copy),
                     reads=[('ptb', 1)], writes=[('Bt', par)])
                x3 = ptb[:, 1152:1408].rearrange("p (h q) -> p h q", h=4)
                S.op('dve', lambda e: e.tensor_tensor(out=xw[par][:].rearrange("p (h q) -> p h q", h=4), in0=x3,
                                                      in1=w2[:, c, hs].unsqueeze(2).broadcast_to([128, 4, 64]), op=ALU.mult),
                     reads=[('ptb', 1), 'w2'], writes=[('xw', par)])
                if not main:
                    return
                S.op('dve', lambda e: e.tensor_tensor(out=xdt[par][:].rearrange("p (h q) -> p h q", h=4), in0=x3,
                                                      in1=dtt[:, c, hs].unsqueeze(2).broadcast_to([128, 4, 64]), op=ALU.mult),
                     reads=[('ptb', 1), 'dtt'], writes=[('xdt', par)])
                S.op('act', lambda e: e.activation(out=xtm[par][:], in_=ptb[:, 1152:1408], func=AF.Copy),
                     reads=[('ptb', 1)], writes=[('xtm', par)])
                S.op('pe', lambda e: e.matmul(S2[:, 0:128], lhsT=Bfm[bs][:, cs], rhs=Cfm[bs][:, cs], start=True, stop=True),
                     reads=[('Bfm', bs), ('Cfm', bs)], writes=['S2'])
                S.op('dve', lambda e: e.tensor_tensor(out=CBm[par][:], in0=S2[:, 0:128], in1=LT[:], op=ALU.mult),
                     reads=['S2', 'LT'], writes=[('CBm', par)])
                S.op('dve', lambda e: e.tensor_tensor(out=Rt[par][:], in0=adthl[:, :, c, hs].unsqueeze(3).broadcast_to([128, 2, 4, 128]),
                                                      in1=LTb[:].unsqueeze(1).unsqueeze(1).broadcast_to([128, 2, 4, 128]), op=ALU.mult),
                     reads=['adth', 'adtl', 'LTb'], writes=[('Rt', par)])
                for t in range(2):
                    Rf = Rt[par][:, t].rearrange("p h l -> p (h l)")
                    S.op('pe', lambda e: e.matmul(S0[:], lhsT=Ub[:], rhs=Rf, start=(t == 0), stop=(t == 1)),
                         reads=['Ub', ('Rt', par)], writes=['S0'])
                for t in range(2):
                    Rf = Rt[par][:, t].rearrange("p h l -> p (h l)")
                    S.op('pe', lambda e: e.matmul(S1[:], lhsT=onesb[:], rhs=Rf, start=(t == 0), stop=(t == 1)),
                         reads=['onesb', ('Rt', par)], writes=['S1'])
                S.op('act', lambda e: e.activation(out=E1[par][:], in_=S0[:], func=AF.Exp), reads=['S0'], writes=[('E1', par)])
                S.op('act', lambda e: e.activation(out=E2[par][:], in_=S1[:], func=AF.Exp), reads=['S1'], writes=[('E2', par)])
                S.op('dve', lambda e: e.tensor_tensor(out=MT[par][:], in0=E1[par][:].rearrange("p (h l) -> p h l", h=4),
                                                      in1=CBm[par][:].unsqueeze(1).broadcast_to([128, 4, 128]), op=ALU.mult),
                     reads=[('E1', par), ('CBm', par)], writes=[('MT', par)])
                S.op('dve', lambda e: e.tensor_tensor(out=CE[par][:], in0=E2[par][:].rearrange("p (h l) -> p h l", h=4),
                                                      in1=Cfm[bs][:, cs].unsqueeze(1).broadcast_to([128, 4, 128]), op=ALU.mult),
                     reads=[('E2', par), ('Cfm', bs)], writes=[('CE', par)])

            def back(g, c, par):
                bs = g % 2
                cs = slice(c * 128, c * 128 + 128)
                hs = slice(4 * g, 4 * g + 4)
                if main:
                    a_y, k_y = next_acc()
                    for h in range(4):
                        S.op('pe', lambda e: e.matmul(a_y[:, h * 64:h * 64 + 64], lhsT=MT[par][:, h, :], rhs=xdt[par][:, h * 64:h * 64 + 64],
                                                      start=True, stop=False),
                             reads=[('MT', par), ('xdt', par)], writes=[k_y])
                        S.op('pe', lambda e: e.matmul(a_y[:, h * 64:h * 64 + 64], lhsT=CE[par][:, h, :],
                                                      rhs=Sb[:, (4 * g + h) * 64:(4 * g + h) * 64 + 64], start=False, stop=True),
                             reads=[('CE', par), ('Sb', g)], writes=[k_y])
                    yvp, ynp, smp = yv[par], yn[par], sm[par]
                    S.op('dve', lambda e: e.tensor_tensor(out=yvp[:].rearrange("p (h q) -> p h q", h=4),
                                                          in0=xtm[par][:].rearrange("p (h q) -> p h q", h=4),
                                                          in1=bc[:, B_DSK + 4 * g:B_DSK + 4 * g + 4].unsqueeze(2).broadcast_to([128, 4, 64]),
                                                          op=ALU.mult), reads=[('xtm', par), 'bc'], writes=[('yv', par)])
                    S.op('dve', lambda e: e.tensor_tensor(out=yvp[:], in0=a_y[:, 0:256], in1=yvp[:], op=ALU.add),
                         reads=[k_y, ('yv', par)], writes=[('yv', par)])
                    S.op('dve', lambda e: e.tensor_tensor(out=yvp[:], in0=yvp[:], in1=zs[bs][:, c, :], op=ALU.mult),
                         reads=[('yv', par), ('zs', bs, c // 2)], writes=[('yv', par)])
                    S.op('act', lambda e: e.activation(out=ynp[:], in_=yvp[:], func=AF.Square, accum_out=smp[:, 0:1]),
                         reads=[('yv', par)], writes=[('sm', par), ('yn', par)])
                    S.op('pool', lambda e: e.tensor_scalar(out=smp[:, 1:2], in0=smp[:, 0:1], scalar1=1.0 / 256, scalar2=1e-6,
                                                           op0=ALU.mult, op1=ALU.add), reads=[('sm', par)], writes=[('sm1', par)])
                    S.op('pool', lambda e: e.tensor_tensor(out=smp[:, 2:3], in0=smp[:, 1:2], in1=nhalf[:, 0:1], op=ALU.pow),
                         reads=[('sm1', par), 'nhalf'], writes=[('sm2', par)])
                    S.op('dve', lambda e: e.tensor_scalar(out=ynp[:], in0=yvp[:], scalar1=smp[:, 2:3], scalar2=None, op0=ALU.mult),
                         reads=[('yv', par), ('sm2', par)], writes=[('yn', par)])
                    for i in range(2):
                        S.op('pe', lambda e: e.transpose(out=ptb[:, 512 + i * 128:512 + i * 128 + 128], in_=ynp[:, i * 128:i * 128 + 128],
                                                         identity=identb[:]),
                             reads=[('yn', par), 'identb'], writes=[('ptb', 0)])
                    for i in range(2):
                        S.op('act', lambda e: e.activation(out=slot(16 + 2 * g + i)[:, cs], in_=ptb[:, 512 + i * 128:512 + i * 128 + 128],
                                                           func=AF.Copy, scale=ppc(P_NG + 2 * g + i)),
                             reads=[('ptb', 0), 'pp'], writes=[('slot', 16 + 2 * g + i)])
                a_s, k_s = S2[:, 128:384], 'S2'
                Sg = St[:, 256 * g:256 * g + 256]
                if main:
                    S.op('pe', lambda e: e.matmul(a_s[:, 0:256], lhsT=Bt[par][:], rhs=xw[par][:], start=True, stop=True),
                         reads=[('Bt', par), ('xw', par)], writes=[k_s])
                    cdv = cdb[:, c, hs]
                else:
                    S.op('pe', lambda e: e.matmul(a_s[:, 0:256], lhsT=Bt[par][:], rhs=xw[par][:], start=(c == 0), stop=(c == 3)),
                         reads=[('Bt', par), ('xw', par)], writes=[k_s])
                    if c < 3:
                        return
                    cdv = cdb[:, 0, hs]
                S.op('dve', lambda e: e.tensor_tensor(out=Sg.rearrange("p (h q) -> p h q", h=4),
                                                      in0=Sg.rearrange("p (h q) -> p h q", h=4),
                                                      in1=cdv.unsqueeze(2).broadcast_to([128, 4, 64]), op=ALU.mult),
                     reads=[('St', g), 'cdb'], writes=[('St', g)])
                S.op('dve', lambda e: e.tensor_tensor(out=Sg, in0=a_s[:, 0:256], in1=Sg, op=ALU.add),
                     reads=[k_s, ('St', g)], writes=[('St', g)])
                if main:
                    S.op('act', lambda e: e.activation(out=Sb[:, 256 * g:256 * g + 256], in_=Sg, func=AF.Copy),
                         reads=[('St', g)], writes=[('Sb', g)])

            wcur = g1_load(0)
            for part in range(4):
                g1_part(0, part, *wcur)
            for g in range(8):
                bstep(g)
                if g < 7:
                    wnext = g1_load(g + 1)
                front(g, 0, 0)
                for c in range(4):
                    if c < 3:
                        front(g, c + 1, (c + 1) % 2)
                    back(g, c, c % 2)
                    if g < 7:
                        g1_part(g + 1, c, *wnext)

            if not main:
                continue

            if upto < 'D':
                continue
            S.tag = 'MD'
            for i in range(8):
                if i % 2 == 0:
                    wsv, wsk = load_w(w_so, [(i * 128, 256)], 16)
                if i % 4 == 0:
                    wgv, wgk = load_w(w_in, [(OFF_GATE + 1024 + i * 128, 512)], 8)
                a_p, k_p = next_acc()
                a_g, k_g = next_acc()
                mm_fm(a_p[:], k_p, wsv, wsk, (i % 2) * 128, lambda kt: (slot(16 + kt), [('slot', 16 + kt)]), 16)
                mm_fm(a_g[:], k_g, wgv, wgk, (i % 4) * 128, hT_rhs, 8)
                gt, gk = gate_tmp()
                S.op('act', lambda e: e.activation(out=gt[:], in_=a_g[:], func=AF.Tanh,
                                                   bias=ppH[:, P_GATEB + 8 + i:P_GATEB + 9 + i], scale=0.5),
                     reads=[k_g, 'ppH'], writes=[gk])
                S.op('dve', lambda e: e.tensor_scalar(out=gt[:], in0=gt[:], scalar1=0.5, scalar2=0.5,
                                                      op0=ALU.mult, op1=ALU.add), reads=[gk], writes=[gk])
                S.op('dve', lambda e: e.tensor_tensor(out=gt[:], in0=a_p[:], in1=gt[:], op=ALU.mult),
                     reads=[k_p, gk], writes=[gk])
                S.op('dve', lambda e: e.tensor_tensor(out=slot(8 + i), in0=gt[:], in1=slot(8 + i), op=ALU.add),
                     reads=[gk, ('slot', 8 + i)], writes=[('slot', 8 + i)])

            for half in range(2):
                wov, wok = load_w(w_o, [(half * 512, 512)], 8)
                for blk in range(4):
                    a_o, k_o = next_acc()
                    for kt in range(8):
                        S.op('pe', lambda e: e.matmul(a_o[:], lhsT=slot(8 + kt)[:, blk * 128:blk * 128 + 128], rhs=wov[:, kt, :],
                                                      start=(kt == 0), stop=(kt == 7)),
                             reads=[('slot', 8 + kt), wok], writes=[k_o])
                    S.op('act', lambda e: e.activation(out=x1[:, blk, half * 512:half * 512 + 512], in_=a_o[:], func=AF.Copy),
                         reads=[k_o], writes=[('x1', blk)])
                    S.op('act', lambda e: e.activation(out=junk[:, 0:512], in_=a_o[:], func=AF.Square,
                                                       accum_out=ss[:, 4 + blk * 2 + half:5 + blk * 2 + half]),
                         reads=[k_o], writes=[('ss', 4 + blk * 2 + half), 'junk'])
            for blk in range(4):
                S.op('dve', lambda e: e.tensor_tensor(out=ss[:, 12 + blk:13 + blk], in0=ss[:, 4 + 2 * blk:5 + 2 * blk],
                                                      in1=ss[:, 5 + 2 * blk:6 + 2 * blk], op=ALU.add),
                     reads=[('ss', 4 + 2 * blk), ('ss', 5 + 2 * blk)], writes=[('ss', 12 + blk)])
            rstd_cols(12, 4, 1.0 / D, 1e-6)
            for blk in range(4):
                i = state['xst']
                state['xst'] = (i + 1) % 2
                S.dma('sp', xst[i][:], xin[r0 + blk * 128:r0 + blk * 128 + 128, :], writes=[('xst', i)])
                S.op('dve', lambda e: e.scalar_tensor_tensor(out=x1[:, blk, :], in0=x1[:, blk, :], scalar=rs[:, 12 + blk:13 + blk],
                                                             in1=bc[:, B_POSTG:B_POSTG + D], op0=ALU.mult, op1=ALU.mult),
                     reads=[('x1', blk), ('rs', 12 + blk), 'bc'], writes=[('x1', blk)])
                S.op('dve', lambda e: e.tensor_tensor(out=x1[:, blk, :], in0=x1[:, blk, :], in1=xst[i][:], op=ALU.add),
                     reads=[('x1', blk), ('xst', i)], writes=[('x1', blk)])

            if upto < 'E':
                continue
            S.tag = 'ME'
            for blk in range(4):
                src_ap, src_keys = x1[:, blk, :], [('x1', blk)]
                rms_rows(src_ap, src_keys, blk)
                rstd_cols(blk, 1, 1.0 / D, 1e-6)
                for half in range(2):
                    S.op('dve', lambda e: e.tensor_scalar(out=hb[:, half * 512:half * 512 + 512],
                                                          in0=src_ap[:, half * 512:half * 512 + 512],
                                                          scalar1=rs[:, blk:blk + 1], scalar2=None, op0=ALU.mult),
                         reads=src_keys + [('rs', blk)], writes=[('hb', half)])
                for kt in range(8):
                    S.op('pe', lambda e: e.transpose(out=ptb[:, (kt % 4) * 512 + (kt // 4) * 128:(kt % 4) * 512 + (kt // 4) * 128 + 128],
                                                     in_=hb[:, kt * 128:kt * 128 + 128], identity=identb[:]),
                         reads=[('hb', kt // 4), 'identb'], writes=[('ptb', (kt % 4) // 2)])
                for q in range(4):
                    for hh in range(2):
                        kt = q + 4 * hh
                        S.op('dve', lambda e: e.tensor_scalar(out=slot(kt)[:, blk * 128:blk * 128 + 128],
                                                              in0=ptb[:, q * 512 + hh * 128:q * 512 + hh * 128 + 128],
                                                              scalar1=ppc(P_FPREG + kt), scalar2=None, op0=ALU.mult),
                             reads=[('ptb', q // 2), 'pp'], writes=[('slot', kt)])
            for i in range(22):
                if i % 2 == 0:
                    wv, wk = load_w(w_gu, [(i * 128, 256), (FH + i * 128, 256)], 8)
                ii = i % 2
                a_g, k_g = next_acc()
                a_u, k_u = next_acc()
                mm_fm(a_g[:], k_g, wv, wk, ii * 128, hT_rhs, 8)
                mm_fm(a_u[:], k_u, wv, wk, 256 + ii * 128, hT_rhs, 8)
                gt, gk = gate_tmp()
                S.op('act', lambda e: e.activation(out=gt[:], in_=a_g[:], func=AF.Silu), reads=[k_g], writes=[gk])
                S.op('dve', lambda e: e.tensor_tensor(out=slot(8 + i), in0=a_u[:], in1=gt[:], op=ALU.mult),
                     reads=[k_u, gk], writes=[('slot', 8 + i)])
            for q in range(4):
                wv, wk = load_w(w_dn, [(q * 256, 256)], 22)
                for half in range(2):
                    a_o, k_o = next_acc()
                    for bb in range(2):
                        blk = half * 2 + bb
                        for kt in range(22):
                            S.op('pe', lambda e: e.matmul(a_o[:, bb * 256:bb * 256 + 256], lhsT=slot(8 + kt)[:, blk * 128:blk * 128 + 128],
                                                          rhs=wv[:, kt, :], start=(kt == 0), stop=(kt == 21)),
                                 reads=[('slot', 8 + kt), wk], writes=[k_o])
                    for bb in range(2):
                        blk = half * 2 + bb
                        fob, fok = fo_t(blk)
                        S.op('act', lambda e: e.activation(out=fob[:, q * 256:q * 256 + 256], in_=a_o[:, bb * 256:bb * 256 + 256], func=AF.Copy),
                             reads=[k_o], writes=fok)
            for blk in range(4):
                fob, fok = fo_t(blk)
                rms_rows(fob, fok, 8 + blk)
            rstd_cols(8, 4, 1.0 / D, 1e-6)
            for blk in range(4):
                fob, fok = fo_t(blk)
                S.op('dve', lambda e: e.scalar_tensor_tensor(out=fob, in0=fob, scalar=rs[:, 8 + blk:9 + blk],
                                                             in1=bc[:, B_FPOSTG:B_FPOSTG + D], op0=ALU.mult, op1=ALU.mult),
                     reads=fok + [('rs', 8 + blk), 'bc'], writes=fok)
                S.op('dve', lambda e: e.tensor_tensor(out=fob, in0=fob, in1=x1[:, blk, :], op=ALU.add),
                     reads=fok + [('x1', blk)], writes=fok)
                ro = (p - npre) * 512 + blk * 128
                S.dma('sp', outd[ro:ro + 128, :], fob, reads=fok, writes=[('out', p, blk)])
        S.finish('sp')
    return nc


def _prep(inputs):
    f = lambda k: np.ascontiguousarray(np.asarray(inputs[k], dtype=np.float32)[0])
    pp = np.zeros((128, NPP), np.float32)

    def pcol(v, c0):
        n = v.shape[0] // 128
        pp[:, c0:c0 + n] = v.reshape(n, 128).T
    pcol(f("gate_b"), P_GATEB)
    pcol(f("glu_b"), P_GLUB)
    dww = f("conv_dw_w")
    pp[:, P_DWW:P_DWW + 248] = dww.T.reshape(8, 128, 31).transpose(1, 0, 2).reshape(128, 248)
    pcol(f("conv_dw_b"), P_DWB)
    pcol(f("conv_ln_g"), P_LNG)
    pcol(f("conv_ln_b"), P_LNB)
    pcol(f("conv_pw_b"), P_PWB)
    scw = f("ssm_conv_w")
    pp[:, P_SCW:P_SCW + 128] = scw.T.reshape(32, 128, 4).transpose(1, 0, 2).reshape(128, 128)
    pcol(f("ssm_conv_b"), P_SCB)
    pcol(f("ssm_norm_g"), P_NG)
    pcol(f("mix_pre_g"), P_PREG)
    pcol(f("ffn_pre_g"), P_FPREG)
    bc = np.zeros((128, NBC), np.float32)
    bc[:, B_POSTG:B_POSTG + D] = f("mix_post_g")[None, :]
    bc[:, B_FPOSTG:B_FPOSTG + D] = f("ffn_post_g")[None, :]
    bc[:, B_DTB:B_DTB + 32] = f("dt_bias")[None, :]
    bc[:, B_ALOG:B_ALOG + 32] = f("a_log")[None, :]
    bc[:, B_DSK:B_DSK + 32] = f("d_skip")[None, :]
    shared = {"w_in": f("w_in"), "w_pw": f("conv_pw_w"), "w_so": f("ssm_out_w"), "w_o": f("w_out"),
              "w_gu": f("w_gate_up"), "w_dn": f("w_down"), "bc": bc}
    x = np.asarray(inputs["x"], dtype=np.float32)
    in_maps = []
    for core in range(8):
        b, hf = core // 2, core % 2
        xin = np.zeros((4096, D), np.float32)
        if hf == 1:
            xin[:] = x[b]
        else:
            xin[2048:] = x[b, :2048]
        ppc = pp.copy()
        ppc[:, P_FLAG] = float(hf)
        m = dict(shared)
        m["xin"] = xin
        m["pp"] = ppc
        in_maps.append(m)
    return in_maps


def kernel(**inputs):
    in_maps = _prep(inputs)
    nc = build_program()
    res = run_bass_kernel_spmd(nc, in_maps, core_ids=list(range(8)))
    out = np.zeros((4, 4096, D), np.float32)
    for core in range(8):
        b, hf = core // 2, core % 2
        out[b, hf * 2048:(hf + 1) * 2048] = res.results[core]["out"]
    return out
```
